# Optimizing a Trainium2 kernel written in Bass

```python
import math
import jax
import jax.numpy as jnp
from jax import lax
import numpy as np

D_MODEL = 2048
BATCH = 4
SEQ = 2048
DEPTH = 2
DEC_BATCH = 128
DEC_SEQ = 8
PAST_LEN = 16384
PAGE_SIZE = 128

EPS = 1e-6
GLA_WIDTH = D_MODEL // 2
GLA_HEADS = 4
GLA_DV = GLA_WIDTH // GLA_HEADS
GLA_DK = GLA_DV // 2
GLA_LOWRANK = 16
GK_NORM = 16.0
GLA_CHUNK = 64
POOL_WIDTH = D_MODEL - GLA_WIDTH
POOL_WINDOWS = (2, 4, 8, 16)
POOL_GROUPS = 4
POOL_GC = POOL_WIDTH // POOL_GROUPS
POOL_BUF = 15
D_FF = ((8 * D_MODEL // 3 + 255) // 256) * 256
Q_END = GLA_HEADS * GLA_DK
K_END = Q_END + GLA_HEADS * GLA_DK
V_END = K_END + GLA_WIDTH
G_END = V_END + GLA_WIDTH
GK_END = G_END + GLA_LOWRANK
IN_WIDTH = GK_END + POOL_WIDTH

kernel_name = "hymba_gla_pool_decoder_step"


def rmsnorm(x, g):
    xf = x.astype(jnp.float32)
    r = lax.rsqrt(jnp.mean(xf * xf, axis=-1, keepdims=True) + EPS)
    return (xf * r).astype(x.dtype) * g


def gla(q, k, v, gk, s0):
    B, T = q.shape[0], q.shape[1]
    c = math.gcd(T, GLA_CHUNK)
    n = T // c

    def split(a):
        return a.reshape(B, n, c, a.shape[2], a.shape[3]).transpose(1, 0, 3, 2, 4)

    qc, kc, vc, gc = split(q), split(k), split(v), split(gk)
    causal = jnp.tril(jnp.ones((c, c), dtype=bool))[:, :, None]

    def step(S, inp):
        qi, ki, vi, gi = inp
        b = jnp.cumsum(gi.astype(jnp.float32), axis=2)
        diff = b[:, :, :, None, :] - b[:, :, None, :, :]
        decay = jnp.exp(jnp.where(causal, diff, -jnp.inf))
        scores = jnp.einsum('bhid,bhjd,bhijd->bhij', qi.astype(jnp.float32), ki.astype(jnp.float32), decay)
        o_intra = jnp.einsum('bhij,bhjv->bhiv', scores, vi.astype(jnp.float32))
        o_inter = jnp.einsum('bhid,bhdv->bhiv', qi.astype(jnp.float32) * jnp.exp(b), S)
        b_last = b[:, :, -1:, :]
        k_dec = ki.astype(jnp.float32) * jnp.exp(b_last - b)
        S_new = jnp.exp(b_last[:, :, 0, :])[..., None] * S + jnp.einsum('bhjd,bhjv->bhdv', k_dec, vi.astype(jnp.float32))
        return S_new, o_intra + o_inter

    S_fin, o = lax.scan(step, s0.astype(jnp.float32), (qc, kc, vc, gc))
    o = o.transpose(1, 0, 3, 2, 4).reshape(B, T, GLA_HEADS, GLA_DV)
    return o.astype(q.dtype), S_fin.astype(s0.dtype)


def pool_mixer(u, buf, pos0, w_pool, pool_scale):
    B, T = u.shape[0], u.shape[1]
    ext = jnp.concatenate([buf.astype(u.dtype), u], axis=1)
    cs = jnp.cumsum(ext.astype(jnp.float32), axis=1)
    cs = jnp.pad(cs, ((0, 0), (1, 0), (0, 0)))
    pos = pos0 + jnp.arange(T)
    outs = []
    for i, w in enumerate(POOL_WINDOWS):
        lo, hi = i * POOL_GC, (i + 1) * POOL_GC
        s = cs[:, POOL_BUF + 1:POOL_BUF + 1 + T, lo:hi] - cs[:, POOL_BUF + 1 - w:POOL_BUF + 1 - w + T, lo:hi]
        cnt = jnp.minimum(w, pos + 1).astype(jnp.float32)[None, :, None]
        outs.append(s / cnt - u[..., lo:hi].astype(jnp.float32))
    d = jnp.stack(outs, axis=2).astype(u.dtype)
    y = jnp.einsum('btgc,gcd->btgd', d, w_pool).reshape(B, T, POOL_WIDTH) * pool_scale
    return y, ext[:, -POOL_BUF:]


def mixer(h, s_gla, buf, pos0, w_in, w_gk_up, b_gk, gla_norm, w_pool, pool_scale, w_out):
    B, T = h.shape[0], h.shape[1]
    proj = h @ w_in
    q = proj[..., :Q_END].reshape(B, T, GLA_HEADS, GLA_DK) * (GLA_DK ** -0.5)
    k = proj[..., Q_END:K_END].reshape(B, T, GLA_HEADS, GLA_DK)
    v = proj[..., K_END:V_END].reshape(B, T, GLA_HEADS, GLA_DV)
    g = proj[..., V_END:G_END]
    gk_lr = proj[..., G_END:GK_END]
    u = proj[..., GK_END:]
    gk = jax.nn.log_sigmoid((gk_lr @ w_gk_up + b_gk).astype(jnp.float32)) / GK_NORM
    gk = gk.reshape(B, T, GLA_HEADS, GLA_DK)
    o, s_new = gla(q, k, v, gk, s_gla)
    o = rmsnorm(o, gla_norm).reshape(B, T, GLA_WIDTH) * jax.nn.silu(g)
    p, buf_new = pool_mixer(u, buf, pos0, w_pool, pool_scale)
    out = jnp.concatenate([o, p.astype(o.dtype)], axis=-1) @ w_out
    return out, s_new, buf_new


def trunk(x, s_gla_all, buf_all, pos0, norm_mix, w_in, w_gk_up, b_gk, gla_norm, w_pool,
          pool_scale, w_out, norm_ffn, w_gate, w_up, w_down, norm_final):
    new_s, new_b = [], []
    for l in range(DEPTH):
        h = rmsnorm(x, norm_mix[l])
        m, s, b = mixer(h, s_gla_all[l], buf_all[l], pos0, w_in[l], w_gk_up[l], b_gk[l],
                        gla_norm[l], w_pool[l], pool_scale[l], w_out[l])
        x = x + m
        h = rmsnorm(x, norm_ffn[l])
        x = x + (jax.nn.silu(h @ w_gate[l]) * (h @ w_up[l])) @ w_down[l]
        new_s.append(s)
        new_b.append(b)
    return rmsnorm(x, norm_final), jnp.stack(new_s), jnp.stack(new_b)


def setup_inputs(seed: int = 0) -> dict:
    key = jax.random.key(seed)
    ks = jax.random.split(key, 20)
    f = jnp.float32
    nrm = lambda k, shape, s: jax.random.normal(k, shape, f) * s
    return {
        "x_prompt": nrm(ks[0], (BATCH, SEQ, D_MODEL), 1.0),
        "x_sample": nrm(ks[1], (DEC_BATCH, DEC_SEQ, D_MODEL), 1.0),
        "state_gla": nrm(ks[2], (DEPTH, DEC_BATCH, GLA_HEADS, GLA_DK, GLA_DV), 0.5),
        "state_pool": nrm(ks[3], (DEPTH, DEC_BATCH, POOL_BUF, POOL_WIDTH), 1.0),
        "norm_mix": 1.0 + nrm(ks[4], (DEPTH, D_MODEL), 0.01),
        "w_in": nrm(ks[5], (DEPTH, D_MODEL, IN_WIDTH), D_MODEL ** -0.5),
        "w_gk_up": nrm(ks[6], (DEPTH, GLA_LOWRANK, GLA_HEADS * GLA_DK), GLA_LOWRANK ** -0.5),
        "b_gk": nrm(ks[7], (DEPTH, GLA_HEADS * GLA_DK), 0.01),
        "gla_norm": 1.0 + nrm(ks[8], (DEPTH, GLA_DV), 0.01),
        "w_pool": nrm(ks[9], (DEPTH, POOL_GROUPS, POOL_GC, POOL_GC), POOL_GC ** -0.5),
        "pool_scale": 1.0 + nrm(ks[10], (DEPTH, POOL_WIDTH), 0.01),
        "w_out": nrm(ks[11], (DEPTH, D_MODEL, D_MODEL), D_MODEL ** -0.5),
        "norm_ffn": 1.0 + nrm(ks[12], (DEPTH, D_MODEL), 0.01),
        "w_gate": nrm(ks[13], (DEPTH, D_MODEL, D_FF), D_MODEL ** -0.5),
        "w_up": nrm(ks[14], (DEPTH, D_MODEL, D_FF), D_MODEL ** -0.5),
        "w_down": nrm(ks[15], (DEPTH, D_FF, D_MODEL), D_FF ** -0.5),
        "norm_final": 1.0 + nrm(ks[16], (D_MODEL,), 0.01),
    }


def reference(x_prompt, x_sample, state_gla, state_pool, norm_mix, w_in, w_gk_up, b_gk,
              gla_norm, w_pool, pool_scale, w_out, norm_ffn, w_gate, w_up, w_down, norm_final):
    s0_prompt = jnp.zeros((DEPTH, BATCH, GLA_HEADS, GLA_DK, GLA_DV), x_prompt.dtype)
    b0_prompt = jnp.zeros((DEPTH, BATCH, POOL_BUF, POOL_WIDTH), x_prompt.dtype)
    y_prompt, state_gla_prompt, state_pool_prompt = trunk(
        x_prompt, s0_prompt, b0_prompt, 0, norm_mix, w_in, w_gk_up, b_gk, gla_norm, w_pool,
        pool_scale, w_out, norm_ffn, w_gate, w_up, w_down, norm_final)
    y_sample, state_gla_sample, state_pool_sample = trunk(
        x_sample, state_gla, state_pool, PAST_LEN, norm_mix, w_in, w_gk_up, b_gk, gla_norm, w_pool,
        pool_scale, w_out, norm_ffn, w_gate, w_up, w_down, norm_final)
    return (y_prompt, y_sample, state_gla_prompt, state_pool_prompt, state_gla_sample, state_pool_sample)
```

```python
import contextlib
import numpy as np
import ml_dtypes
import concourse.bass as bass
import concourse.mybir as mybir
from concourse.bass_utils import run_bass_kernel_spmd

F32 = mybir.dt.float32
BF16 = mybir.dt.bfloat16
AF = mybir.ActivationFunctionType
ALU = mybir.AluOpType

NCORES = 8
T = 1152
NT = 9
D = 2048
KC = 16
DFF = 5632
NFF = 44
FG = 4
NG = NFF // FG
H = 4
DK = 128
DV = 256
EPS = 1e-6
C_Q, C_K, C_V, C_G, C_GK, C_U = 0, 512, 1024, 2048, 3072, 3088
INW = 4112


class _Op:
    __slots__ = ("eng", "fn", "deps", "dma", "slot", "sig", "sigval", "idx", "inc")


class Sched:
    def __init__(self, nc):
        self.nc = nc
        self.ops = []
        self.lw = {}
        self.rd = {}
        self.bar = {}
        self.last_on = {}
        self.dma_since_bar = []

    def barrier(self):
        deps = set(self.last_on.values()) | set(self.dma_since_bar)
        for e in ("pe", "act", "dve", "pool", "sp"):
            self.bar[e] = set(deps) | self.bar.get(e, set())
        self.dma_since_bar = []

    def add(self, eng, fn, reads=(), writes=(), dma=False, slot=None, inc=16, nobar=False):
        op = _Op()
        op.eng, op.fn, op.dma, op.slot = eng, fn, dma, slot
        op.inc = inc
        op.idx = len(self.ops)
        op.sig = False
        op.sigval = 0
        deps = set()
        ops = self.ops
        for r in reads:
            w = self.lw.get(r)
            if w is not None:
                deps.add(w)
        for wkey in writes:
            w = self.lw.get(wkey)
            if w is not None:
                o = ops[w]
                if o.dma and dma:
                    pass
                elif o.dma or dma or o.eng != eng or eng != "pe":
                    deps.add(w)
            for x in self.rd.get(wkey, ()):
                o = ops[x]
                if o.dma or dma or o.eng != eng or eng != "pe":
                    deps.add(x)
        b = self.bar.pop(eng, None)
        if b:
            deps |= b
        deps.discard(op.idx)
        red = {}
        for d_ in deps:
            o = ops[d_]
            key = ("s", o.slot) if o.dma else ("e", o.eng)
            if key not in red or red[key] < d_:
                red[key] = d_
        op.deps = set(red.values())
        ops.append(op)
        for r in reads:
            self.rd.setdefault(r, []).append(op.idx)
        for wkey in writes:
            self.lw[wkey] = op.idx
            self.rd[wkey] = []
        if dma and not nobar:
            self.dma_since_bar.append(op.idx)
        else:
            self.last_on[eng] = op.idx
        return op.idx

    def emit(self):
        nc = self.nc
        ops = self.ops
        for op in ops:
            for d in op.deps:
                ops[d].sig = True
        engs = ["pe", "act", "dve", "pool", "sp"]
        cnt = {e: 0 for e in engs}
        slotcnt = {}
        for op in ops:
            if op.dma:
                slotcnt[op.slot] = slotcnt.get(op.slot, 0) + op.inc
                op.sigval = slotcnt[op.slot]
            elif op.sig:
                cnt[op.eng] += 1
                op.sigval = cnt[op.eng]
        slots = sorted(slotcnt.keys(), key=str)
        with contextlib.ExitStack() as es:
            esem = {e: es.enter_context(nc.semaphore("se_" + e)) for e in engs}
            ssem = {s: es.enter_context(nc.semaphore("sd_%d" % i)) for i, s in enumerate(slots)}
            block = es.enter_context(nc.Block())
            per_eng = {e: [op for op in ops if op.eng == e] for e in engs}

            def run(e, engobj):
                waited = {}
                for op in per_eng[e]:
                    need = {}
                    for d in op.deps:
                        o = ops[d]
                        if o.dma:
                            key = ("s", o.slot)
                            sem = ssem[o.slot]
                        else:
                            if o.eng == e and not op.dma and False:
                                continue
                            key = ("e", o.eng)
                            sem = esem[o.eng]
                        v = o.sigval
                        if waited.get(key, 0) >= v:
                            continue
                        if key not in need or need[key][1] < v:
                            need[key] = (sem, v)
                    for key, (sem, v) in need.items():
                        engobj.wait_ge(sem, v)
                        waited[key] = v
                    ins = op.fn(engobj)
                    if op.dma:
                        ins.then_inc(ssem[op.slot], op.inc)
                    elif op.sig:
                        ins.then_inc(esem[op.eng], 1)
                if e == "sp":
                    for s in slots:
                        if waited.get(("s", s), 0) < slotcnt[s]:
                            engobj.wait_ge(ssem[s], slotcnt[s])

            @block.tensor
            def _(eng):
                run("pe", eng)

            @block.scalar
            def _(eng):
                run("act", eng)

            @block.vector
            def _(eng):
                run("dve", eng)

            @block.gpsimd
            def _(eng):
                run("pool", eng)

            @block.sync
            def _(eng):
                run("sp", eng)
        return dict(n_ops=len(ops), n_sig=dict(cnt), n_slots=len(slots))


class _Stop(Exception):
    pass


def build_program(depth=2, stage=99):
    nc = bass.Bass("TRN2", target_bir_lowering=False)
    di = lambda n, s, dt=F32: nc.dram_tensor(n, s, dt, kind="ExternalInput").ap()
    do = lambda n, s, dt=F32: nc.dram_tensor(n, s, dt, kind="ExternalOutput").ap()
    xin = di("xin", [T, D])
    sgla = di("sgla", [2, 16, H, DK, DV])
    spool = di("spool", [2, 16, 15, 1024])
    norm_mix = di("norm_mix", [2, D])
    w_in = di("w_in", [2, D, INW])
    w_gk_up = di("w_gk_up", [2, 16, 512])
    b_gk = di("b_gk", [2, 512])
    gla_norm = di("gla_norm", [2, DV])
    w_pool = di("w_pool", [2, 4, 256, 256])
    pool_scale = di("pool_scale", [2, 1024])
    w_out = di("w_out", [2, D, D])
    norm_ffn = di("norm_ffn", [2, D])
    if stage >= 6:
        w_gate = di("w_gate", [2, D, DFF])
        w_up = di("w_up", [2, D, DFF])
        w_down = di("w_down", [2, DFF, D])
    norm_final = di("norm_final", [1, D])
    c_identb = di("c_identb", [128, 128], BF16)
    c_identf = di("c_identf", [128, 128])
    c_tri = di("c_tri", [128, 256])
    c_seqsel = di("c_seqsel", [128, 32])
    c_rowmask = di("c_rowmask", [128, 16])
    c_flag = di("c_flag", [128, 1])
    c_invc = di("c_invc", [128, 60])
    y = do("y", [T, D])
    sgp = do("sgp", [2, H, DK, DV])
    spp = do("spp", [2, 15, 1024])
    sgs = do("sgs", [2, 16, H, DK, DV])
    sps = do("sps", [2, 16, 15, 1024])
    cc_u_in = [nc.dram_tensor("cc_u_in%d" % l, [16, 1024], F32).ap() for l in range(2)]
    cc_u_out = [nc.dram_tensor("cc_u_out%d" % l, [32, 1024], F32).ap() for l in range(2)]
    cc_s_in = [[nc.dram_tensor("cc_s_in%d_%d" % (l, h), [128, 256], F32).ap() for h in range(H)] for l in range(2)]
    cc_s_out = [[nc.dram_tensor("cc_s_out%d_%d" % (l, h), [256, 256], F32).ap() for h in range(H)] for l in range(2)]
    GROUPS = [[2 * i, 2 * i + 1] for i in range(NCORES // 2)]

    es = contextlib.ExitStack()
    with es:
        sb = lambda name, shape, dt: es.enter_context(nc.sbuf_tensor(name, shape, dt))
        X = sb("X", [128, NT * D], F32)
        HT = sb("HT", [128, KC * T], BF16)
        MT = sb("MT", [128, 8 * T], BF16)
        WB = [sb("WB0", [128, KC * 512], BF16), sb("WB1", [128, KC * 512], BF16)]
        IDB = sb("IDB", [128, 128], BF16)
        IDF = sb("IDF", [128, 128], F32)
        TRI = sb("TRI", [128, 256], F32)
        SEQSEL = sb("SEQSEL", [128, 32], F32)
        ROWMASK = sb("ROWMASK", [128, 16], F32)
        FLAG = sb("FLAG", [128, 1], F32)
        INVC = sb("INVC", [128, 60], F32)
        PSC = sb("PSC", [128, 8], F32)
        GNB = sb("GNB", [128, 256], F32)
        WGKLR = sb("WGKLR", [128, KC * 16], BF16)
        STAT = sb("STAT", [128, 48], F32)
        R1W = 11736
        R1 = sb("R1", [128, R1W], F32)
        PS = [es.enter_context(nc.psum_tensor("PS%d" % i, [128, 512], F32)) for i in range(8)]

        def r1f(off, n, parts=128):
            return R1[0:parts, off:off + n]

        def r1b(off, n, parts=128):
            return R1[0:parts, off:off + n].bitcast(BF16)

        Xv = X[:].rearrange("p (t d) -> p t d", d=D)
        HTv = HT[:].rearrange("p (k t) -> p k t", t=T)
        MTv = MT[:].rearrange("p (k t) -> p k t", t=T)
        GBC = MT[:, 0:2 * D].bitcast(F32)

        S = Sched(nc)
        A = S.add

        def dma(q, out, in_, reads, writes, slot, slow=False, nobar=False):
            if slow:
                A(q, lambda e: e.dma_start(out=out, in_=in_, allow_slow_non_contiguous=True), reads=reads, writes=writes, dma=True, slot=slot, nobar=nobar)
            else:
                A(q, lambda e: e.dma_start(out=out, in_=in_), reads=reads, writes=writes, dma=True, slot=slot, nobar=nobar)

        def mm(out, lhsT, rhs, start, stop, reads, writes):
            A("pe", lambda e: e.matmul(out, lhsT=lhsT, rhs=rhs, start=start, stop=stop), reads=reads, writes=writes)

        def tr(out, in_, reads, writes):
            A("pe", lambda e: e.transpose(out, in_, IDB[:]), reads=list(reads) + ["IDB"], writes=writes)

        def act(out, in_, func, reads, writes, scale=1.0, bias=0.0, accum=None):
            if accum is None:
                A("act", lambda e: e.activation(out=out, in_=in_, func=func, bias=bias, scale=scale), reads=reads, writes=writes)
            else:
                A("act", lambda e: e.activation(out=out, in_=in_, func=func, bias=bias, scale=scale, accum_out=accum), reads=reads, writes=writes)

        def tt(eng, out, in0, in1, op, reads, writes):
            A(eng, lambda e: e.tensor_tensor(out=out, in0=in0, in1=in1, op=op), reads=reads, writes=writes)

        def ts(eng, out, in0, s1, op0, reads, writes):
            if op0 == ALU.mult:
                A(eng, lambda e: e.tensor_scalar_mul(out, in0, s1), reads=reads, writes=writes)
            else:
                A(eng, lambda e: e.tensor_scalar_add(out, in0, s1), reads=reads, writes=writes)

        def stt(eng, out, in0, scalar, in1, op0, op1, reads, writes):
            A(eng, lambda e: e.scalar_tensor_tensor(out=out, in0=in0, scalar=scalar, in1=in1, op0=op0, op1=op1), reads=reads, writes=writes)

        def cp(eng, out, in_, reads, writes):
            if eng == "act":
                A("act", lambda e: e.copy(out=out, in_=in_), reads=reads, writes=writes)
            else:
                A(eng, lambda e: e.tensor_copy(out=out, in_=in_), reads=reads, writes=writes)

        def memset(eng, ap, val, writes):
            A(eng, lambda e: e.memset(ap, val), writes=writes)

        xk = lambda t: [("x", t, cb) for cb in range(4)]
        htk = lambda tiles: [("hT", t) for t in tiles]
        TB = [(0, 384), (384, 384), (768, 384)]
        tb_tiles = lambda tb: [3 * tb, 3 * tb + 1, 3 * tb + 2]

        dma("sp", IDB[:], c_identb, [], ["IDB"], "c0")
        dma("sp", IDF[:], c_identf, [], ["IDF"], "c1")
        dma("sp", TRI[:], c_tri, [], ["TRI"], "c2")
        dma("sp", SEQSEL[:], c_seqsel, [], ["SEQSEL"], "c3")
        dma("sp", ROWMASK[:], c_rowmask, [], ["ROWMASK"], "c4")
        dma("sp", FLAG[:], c_flag, [], ["FLAG"], "c5")
        dma("sp", INVC[:], c_invc, [], ["INVC"], "c6")
        dma("sp", GBC, norm_mix[0, :].partition_broadcast(128), [], ["gbc"], "gbc")
        for t in range(NT):
            dma("sp", Xv[:, t, :], xin[t * 128:(t + 1) * 128, :], [], xk(t), ("xin", t), nobar=True)

        XN_OFF, JUNK_OFF, YT_OFF = 0, 1024, 2048

        def emit_norm(gvec, to_y=False, skip_gbc=False):
            S.barrier()
            if not skip_gbc:
                dma("sp", GBC, gvec[0, :].partition_broadcast(128), [], ["gbc"], "gbc")
            XNb = [r1b(8192, 1024), r1b(9216, 1024)]
            junk = r1b(10240, 1024)

            def front(t):
                p = t % 2
                act(junk, Xv[:, t, :], AF.Square, xk(t), ["junk", ("ss", p)], scale=1.0 / np.sqrt(D), accum=STAT[:, p:p + 1])
                act(STAT[:, 2 + p:3 + p], STAT[:, p:p + 1], AF.Ln, [("ss", p)], [("lnv", p)], bias=EPS)
                act(STAT[:, 4 + p:5 + p], STAT[:, 2 + p:3 + p], AF.Exp, [("lnv", p)], [("rs", p)], scale=-0.5)
                rs = STAT[:, 4 + p:5 + p]
                if to_y:
                    yt = r1f(YT_OFF + p * D, D)
                    stt("dve", yt, Xv[:, t, :], rs, GBC, ALU.mult, ALU.mult, xk(t) + [("rs", p), "gbc"], [("yt", p)])
                    dma("sp", y[t * 128:(t + 1) * 128, :], yt, [("yt", p)], [], ("yout", p))
                else:
                    stt("dve", XNb[p], Xv[:, t, :], rs, GBC, ALU.mult, ALU.mult, xk(t) + [("rs", p), "gbc"], [("xn", p)])

            def back(t):
                p = t % 2
                xn = XNb[p]
                for half in range(2):
                    bank = 4 + 2 * p + half
                    pb = PS[bank][:].bitcast(BF16)
                    for j in range(8):
                        k = half * 8 + j
                        tr(pb[:, j * 128:(j + 1) * 128], xn[:, k * 128:(k + 1) * 128], [("xn", p)], [("ps", bank)])
                    dst = HTv[:, half * 8:(half + 1) * 8, t * 128:(t + 1) * 128]
                    src = pb.rearrange("p (k c) -> p k c", c=128)
                    cp("act" if half == 0 else "dve", dst, src, [("ps", bank)], [("hT", t)])

            front(0)
            for t in range(NT):
                if t + 1 < NT:
                    front(t + 1)
                if not to_y:
                    back(t)
            S.barrier()

        def load_w_cols(slot, dst_col0, src2d, ncols, nk=KC):
            pass

        def wb_view(slot, width, nk=KC):
            return WB[slot][:, 0:nk * width].rearrange("p (k n) -> p k n", n=width)

        def load_head_w(l, h, alias=False):
            hb = h % 2
            wa = WB[0][:, hb * 4096:(hb + 1) * 4096].rearrange("p (k n) -> p k n", n=256)
            wbv = wb_view(1, 512)
            wak = ("wb", 0, hb)
            ka = [wak] + ([("wb", 0), ("wb4", 0), ("wb4", 1)] if alias else [])
            kb = [("wb", 1)] + ([("wb4", 2), ("wb4", 3)] if alias else [])
            srcs = [(wa, 0, C_Q + h * 128, 128, ka, wak), (wa, 128, C_K + h * 128, 128, ka, wak),
                    (wbv, 0, C_V + h * 256, 256, kb, ("wb", 1)), (wbv, 256, C_G + h * 256, 256, kb, ("wb", 1))]
            for (dst, d0, c0, cn, wkeys, slot) in srcs:
                dma("pool", dst[:, :, d0:d0 + cn], w_in[l, :, c0:c0 + cn].rearrange("(k p) n -> p k n", p=128),
                    [], wkeys, slot, nobar=True)

        def layer_prefetch(l):
            dma("sp", PSC[:], pool_scale[l, :].rearrange("(k p) -> p k", p=128), [], ["PSC"], "psc", slow=True)
            dma("sp", GNB[:], gla_norm[l, :].partition_broadcast(128), [], ["GNB"], "gnb")
            dma("pool", WGKLR[:].rearrange("p (k n) -> p k n", n=16),
                w_in[l, :, C_GK:C_GK + 16].rearrange("(k p) n -> p k n", p=128), [], ["WGKLR"], "wgklr")
            load_head_w(l, 0, alias=True)

        WD_OFF = [0, 4096]

        def gu_view(slot):
            return WB[slot // 2][:, (slot % 2) * 4096:(slot % 2 + 1) * 4096].rearrange("p (k n) -> p k n", n=256)

        def ffn_slot(ff):
            return (ff + 2) % 4

        def ffn_loads(l, gi, alias=False, part=None):
            b = gi % 2
            for f in range(FG):
                if part == 0 and f >= 2:
                    continue
                if part == 1 and f < 2:
                    continue
                ff = gi * FG + f
                slot = ffn_slot(ff)
                wv = gu_view(slot)
                wk = [("wb4", slot)] + (([("wb", 0), ("wb", 0, slot)] if slot < 2 else [("wb", 1)]) if alias else [])
                dma("pool", wv[:, :, 0:128], w_gate[l, :, ff * 128:(ff + 1) * 128].rearrange("(k p) n -> p k n", p=128),
                    [], wk, ("wb4", slot), nobar=True)
                dma("pool", wv[:, :, 128:256], w_up[l, :, ff * 128:(ff + 1) * 128].rearrange("(k p) n -> p k n", p=128),
                    [], wk, ("wb4", slot), nobar=True)
            if part == 0:
                return
            wd = r1b(WD_OFF[b], 4096).rearrange("p (f n) -> p f n", n=D)
            dma("pool", wd, w_down[l, gi * FG * 128:(gi + 1) * FG * 128, :].rearrange("(f p) n -> p f n", p=128),
                [], [("wd", b)], ("wd", b), nobar=True)

        def chk(n):
            if stage <= n:
                raise _Stop()

        layer_prefetch(0)
        final_done = False
        try:
          for l in range(depth):
            chk(0)
            emit_norm(norm_mix[l:l + 1, :], skip_gbc=(l == 0))
            chk(1)
            GKL_OFF = 0
            WGK_OFF = 576
            ORAW_OFF = 832
            GGB_OFF = 2880
            QS_OFF = 3904
            SST_OFF = 4416
            SBF_OFF = 4928
            S0F_OFF = 5184
            S0B_OFF = 5696
            Z_OFF = 5952
            KM_OFF = 7040
            V_OFF = 8064
            GGT_OFF = 8448
            SP_OFF = 8704
            EQ_OFF = 8832
            EK_OFF = 8960
            QKT_OFF = 9088
            QKTT_OFF = 9344
            SCM_OFF = 9600
            JK_OFF = 9664
            OG_OFF = 9792
            SN_OFF = 10048
            OF_OFF = 10304
            SM_OFF = 10816
            SMB_OFF = 11072
            assert SMB_OFF + 128 <= R1W

            GKL = r1b(GKL_OFF, 576, 17)
            WGK = r1b(WGK_OFF, 256, 17)
            WGKF = r1f(ORAW_OFF, 512, 17)
            memset("dve", GKL, 1.0, ["GKL"])
            dma("sp", WGKF[0:16, :], w_gk_up[l], [], ["WGKF"], "wgkf")
            dma("sp", WGKF[16:17, :], b_gk[l:l + 1, :], [], ["WGKF"], "wgkf")
            cp("dve", WGK, WGKF, ["WGKF"], ["WGK"])
            wlr = WGKLR[:].rearrange("p (k n) -> p k n", n=16)
            for tb, (c0, cn) in enumerate(TB):
                for k in range(KC):
                    mm(PS[2][0:16, 0:cn], wlr[:, k, :], HTv[:, k, c0:c0 + cn], k == 0, k == KC - 1,
                       htk(tb_tiles(tb)) + ["WGKLR"], [("ps", 2)])
                cp("act", GKL[0:16, c0:c0 + cn], PS[2][0:16, 0:cn], [("ps", 2)], ["GKL"])

            Zfull = r1b(Z_OFF, 1088)
            memset("pool", Zfull, 0.0, ["Z"])
            Zdiag = Zfull.rearrange("p (s c) -> p s c", c=136)[:, :, 0:8]
            Zv = Zfull[:, 0:2048].rearrange("p (s t) -> p s t", t=128)
            KMv = r1b(KM_OFF, 1024).rearrange("p (s d) -> p s d", d=128)
            S.barrier()
            SST = r1f(SST_OFF, 512).rearrange("p (h v) -> p h v", v=256)
            SBF = r1b(SBF_OFF, 256).rearrange("p (h v) -> p h v", v=256)
            ORAW = r1f(ORAW_OFF, 2048).rearrange("p (t v) -> p t v", v=256)
            GGB = r1b(GGB_OFF, 1024).rearrange("p (t v) -> p t v", v=256)
            QS = r1b(QS_OFF, 512)
            Vb = [r1b(V_OFF + i * 128, 128) for i in range(3)]
            QKT = [r1b(QKT_OFF + i * 128, 128) for i in range(2)]
            QKTT = [r1b(QKTT_OFF + i * 128, 128) for i in range(2)]
            OGb = [r1b(OG_OFF + i * 128, 128) for i in range(2)]
            OFb = [r1f(OF_OFF + i * 256, 256) for i in range(2)]
            ggt = r1f(GGT_OFF, 256)
            sp_ = r1f(SP_OFF, 128)
            eq = r1f(EQ_OFF, 128)
            ek = r1f(EK_OFF, 128)
            scm = r1b(SCM_OFF, 64)
            ELs = STAT[:, 16:32]
            V8 = r1b(JK_OFF, 128)
            GGB9 = r1b(SM_OFF, 128)
            scm8 = r1b(SM_OFF + 128, 64)

            def epi_front(osrc, okeys, gg, ggkeys, p):
                og = OGb[p]
                c = 8 + 3 * p
                act(og, osrc, AF.Square, okeys, [("og", p), ("e_ss", p)], scale=1.0 / 16.0, accum=STAT[:, c:c + 1])
                act(STAT[:, c + 1:c + 2], STAT[:, c:c + 1], AF.Ln, [("e_ss", p)], [("e_ln", p)], bias=EPS)
                act(STAT[:, c + 2:c + 3], STAT[:, c + 1:c + 2], AF.Exp, [("e_ln", p)], [("e_rs", p)], scale=-0.5)
                stt("dve", og, osrc, STAT[:, c + 2:c + 3], gg, ALU.mult, ALU.mult, okeys + [("e_rs", p)] + ggkeys, [("og", p)])

            def epi_back(h, t, p):
                og = OGb[p]
                pb = PS[5][:, 256 + 128 * p:384 + 128 * p].bitcast(BF16)
                for j in range(2):
                    tr(pb[:, j * 128:(j + 1) * 128], og[:, j * 128:(j + 1) * 128], [("og", p)], [("ps", 5)])
                cp("act", MTv[:, 2 * h:2 * h + 2, t * 128:(t + 1) * 128], pb.rearrange("p (k c) -> p k c", c=128),
                   [("ps", 5)], [("mT", 2 * h, t), ("mT", 2 * h + 1, t)])

            def epilogue(osrc, okeys, gg, ggkeys, h, t, p):
                epi_front(osrc, okeys, gg, ggkeys, p)
                epi_back(h, t, p)

            smb = r1b(SMB_OFF, 128)
            PC = [(6, PS[6][:, 256:512]), (7, PS[7][:, 256:512])]

            def corr_front(hp, t):
                bank, reg = PC[t % 2]
                mm(reg, QS[:, t * 128:(t + 1) * 128], smb, True, True, [("qs", t), "smb"], [("ps", bank)])
                of = OFb[t % 2]
                tt("dve", of, ORAW[:, t, :], reg, ALU.add, [("oraw", t), ("ps", bank)], [("of", t % 2)])
                epi_front(of, [("of", t % 2)], GGB[:, t, :], [("ggb", t)], t % 2)

            def corr_back(hp, t):
                epi_back(hp, t, t % 2)

            pending = None
            for h in range(H):
                hb = h % 2
                wa = WB[0][:, hb * 4096:(hb + 1) * 4096].rearrange("p (k n) -> p k n", n=256)
                wbv = wb_view(1, 512)
                wak = ("wb", 0, hb)
                memset("dve", SST[:, hb, :], 0.0, [("S", hb)])
                memset("pool", SBF[:, hb, :], 0.0, [("Sb", hb)])
                memset("dve", STAT[:, 14:15], 0.0, ["bacc"])

                def tinfo(t):
                    smp = (t == 8)
                    v_ = 1 if smp else 0
                    return smp, v_, (16 if smp else 1), TRI[:, v_ * 128:(v_ + 1) * 128]

                def S1(t):
                    smp, v_, nseq, tri = tinfo(t)
                    pq = PS[2 + t % 2]
                    pvg = PS[t % 2]
                    for k in range(KC):
                        mm(pq[:, 0:256], HTv[:, k, t * 128:(t + 1) * 128], wa[:, k, :], k == 0, k == KC - 1,
                           [("hT", t), wak], [("ps", 2 + t % 2)])
                    mm(pq[:, 256:384], GKL[:, t * 128:(t + 1) * 128], WGK[:, h * 128:(h + 1) * 128], True, True,
                       ["GKL", "WGK"], [("ps", 2 + t % 2)])
                    for k in range(KC):
                        mm(pvg[:], HTv[:, k, t * 128:(t + 1) * 128], wbv[:, k, :], k == 0, k == KC - 1,
                           [("hT", t), ("wb", 1)], [("ps", t % 2)])
                    act(sp_, pq[:, 256:384], AF.Exp, [("ps", 2 + t % 2)], ["sp"], scale=-1.0)
                    act(sp_, sp_, AF.Ln, ["sp"], ["sp"], bias=1.0)
                    V = V8 if smp else Vb[t % 3]
                    cp("act", V, pvg[:, 0:256], [("ps", t % 2)], ["V8" if smp else ("V", t % 3)])
                    act(ggt, pvg[:, 256:512], AF.Exp, [("ps", t % 2)], ["ggt"], scale=-1.0)
                    ts("dve", ggt, ggt, 1.0, ALU.add, ["ggt"], ["ggt"])
                    A("dve", lambda e: e.reciprocal(out=ggt, in_=ggt), reads=["ggt"], writes=["ggt"])
                    tt("dve", ggt, ggt, pvg[:, 256:512], ALU.mult, ["ggt", ("ps", t % 2)], ["ggt"])
                    if smp:
                        tt("dve", GGB9, ggt, GNB[:], ALU.mult, ["ggt", "GNB"], ["ggb9"])
                    else:
                        tt("dve", GGB[:, t, :], ggt, GNB[:], ALU.mult, ["ggt", "GNB"], [("ggb", t)])

                def S2(t):
                    smp, v_, nseq, tri = tinfo(t)
                    pq = PS[2 + t % 2]
                    mm(PS[4][:, 0:128], tri, sp_, True, True, ["TRI", "sp"], [("ps", 4)])
                    sel = SEQSEL[:, v_ * 16:v_ * 16 + nseq]
                    mm(PS[4][:, 128:128 + nseq], sp_, sel, True, True, ["sp", "SEQSEL"], [("ps", 4)])
                    act(eq, PS[4][:, 0:128], AF.Exp, [("ps", 4)], ["eq"], scale=-1.0 / 16.0)
                    act(ek, PS[4][:, 0:128], AF.Exp, [("ps", 4)], ["ek"], scale=1.0 / 16.0)
                    if smp:
                        act(ELs, PS[4][:, 128:144], AF.Exp, [("ps", 4)], ["els"], scale=-1.0 / 16.0)
                    else:
                        act(STAT[:, 6 + t % 2:7 + t % 2], PS[4][:, 128:129], AF.Exp, [("ps", 4)], [("elp", t % 2)], scale=-1.0 / 16.0)
                        act(STAT[:, 32 + t % 2:33 + t % 2], STAT[:, 14:15], AF.Exp, ["bacc"], [("eb", t % 2)], scale=-1.0 / 16.0)
                        tt("dve", STAT[:, 14:15], STAT[:, 14:15], PS[4][:, 128:129], ALU.add, ["bacc", ("ps", 4)], ["bacc"])
                    qkt = QKT[t % 2]
                    stt("dve", qkt[:, 0:128], pq[:, 0:128], float(DK ** -0.5), eq, ALU.mult, ALU.mult,
                        [("ps", 2 + t % 2), "eq"], [("qkt", t % 2)])
                    tt("dve", qkt[:, 128:256], pq[:, 128:256], ek, ALU.mult, [("ps", 2 + t % 2), "ek"], [("qkt", t % 2)])

                def S3(t):
                    smp, v_, nseq, tri = tinfo(t)
                    qkt = QKT[t % 2]
                    qktt = QKTT[t % 2]
                    pbt = PS[5][:, 0:128].bitcast(BF16)
                    tr(pbt[:, 0:128], qkt[:, 0:128], [("qkt", t % 2)], [("ps", 5)])
                    tr(pbt[:, 128:256], qkt[:, 128:256], [("qkt", t % 2)], [("ps", 5)])
                    cp("act", qktt, pbt, [("ps", 5)], [("qktt", t % 2)])
                    if not smp:
                        act(QS[:, t * 128:(t + 1) * 128], pbt[:, 0:128], AF.Copy, [("ps", 5), ("eb", t % 2)], [("qs", t)],
                            scale=STAT[:, 32 + t % 2:33 + t % 2])

                def S4(t):
                    smp, v_, nseq, tri = tinfo(t)
                    qktt = QKTT[t % 2]
                    mm(PS[5][:, 128:256], qktt[:, 128:256], qktt[:, 0:128], True, True, [("qktt", t % 2)], [("ps", 5)])
                    if smp:
                        tt("dve", scm8, PS[5][:, 128:256], tri, ALU.mult, [("ps", 5), "TRI"], ["scm8"])
                    else:
                        tt("dve", scm, PS[5][:, 128:256], tri, ALU.mult, [("ps", 5), "TRI"], ["scm"])

                def S5(t):
                    qkt = QKT[t % 2]
                    qktt = QKTT[t % 2]
                    V = Vb[t % 3]
                    vk = ("V", t % 3)
                    mm(PS[6][:, 0:256], scm, V, True, False, ["scm", vk], [("ps", 6)])
                    mm(PS[6][:, 0:256], qktt[:, 0:128], SBF[:, hb, :], False, True, [("qktt", t % 2), ("Sb", hb)], [("ps", 6)])
                    cp("act", ORAW[:, t, :], PS[6][:, 0:256], [("ps", 6)], [("oraw", t)])
                    mm(PS[7][:, 0:256], qkt[:, 128:256], V, True, True, [("qkt", t % 2), vk], [("ps", 7)])
                    tt("dve", SST[:, hb, :], SST[:, hb, :], PS[7][:, 0:256], ALU.add, [("S", hb), ("ps", 7)], [("S", hb)])
                    ts("dve", SST[:, hb, :], SST[:, hb, :], STAT[:, 6 + t % 2:7 + t % 2], ALU.mult, [("S", hb), ("elp", t % 2)], [("S", hb)])
                    cp("pool", SBF[:, hb, :], SST[:, hb, :], [("S", hb)], [("Sb", hb)])

                t = 8
                S1(t); S2(t); S3(t); S4(t)
                qkt = QKT[t % 2]
                qktt = QKTT[t % 2]
                cp("pool", Zdiag, qktt[:, 0:128].rearrange("p (s j) -> p s j", j=8), [("qktt", t % 2)], ["Z"])
                A("dve", lambda e, qkt=qkt: e.tensor_tensor(
                    out=KMv, in0=qkt[:, 128:256].unsqueeze(1).to_broadcast([128, 16, 128]),
                    in1=ROWMASK[:].unsqueeze(2).to_broadcast([128, 16, 128]), op=ALU.mult),
                  reads=[("qkt", t % 2), "ROWMASK"], writes=["KM"])
                for i in range(8 + 2):
                    if pending is not None and i < 8:
                        corr_front(pending, i)
                    if 0 <= i - 1 < 8:
                        S2(i - 1)
                    if i < 8:
                        S1(i)
                    if pending is not None and i < 8:
                        corr_back(pending, i)
                    if 0 <= i - 1 < 8:
                        S3(i - 1)
                    if 0 <= i - 2 < 8:
                        S5(i - 2)
                    if 0 <= i - 1 < 8:
                        S4(i - 1)

                if h + 1 < H:
                    load_head_w(l, h + 1)
                else:
                    dma("pool", WB[0][:, 0:4096].rearrange("p (k n) -> p k n", n=512),
                        w_out[l, 0:1024, 0:512].rearrange("(k p) n -> p k n", p=128),
                        [], [("wb", 0, 0), ("wb", 0), ("wb4", 0)], ("wb", 0, 0), nobar=True)
                dma("sp", cc_s_in[l][h], SST[:, hb, :], [("S", hb)], [("ccsin", l, h)], "ccsin")
                A("pool", lambda e, l=l, h=h: e.collective_compute("AllGather", ALU.bypass, replica_groups=GROUPS,
                                                                    ins=[cc_s_in[l][h]], outs=[cc_s_out[l][h]]),
                  reads=[("ccsin", l, h)], writes=[("ccsout", l, h)], dma=True, slot=("ccs", l, h), inc=1)

                t = 8
                V = V8
                vk = "V8"
                mm(PS[6][:, 0:256], scm8, V, True, False, ["scm8", vk], [("ps", 6)])
                SNB = [(r1f(SN_OFF, 256), "sn"), (OFb[0], ("of", 0)), (OFb[1], ("of", 1))]
                for s in range(16):
                    b2 = s % 3
                    s0b = r1b(S0B_OFF + b2 * 128, 128) if b2 < 2 else r1b(11456, 128)
                    s0f = r1f(S0F_OFF + b2 * 256, 256) if b2 < 2 else r1f(11200, 256)
                    dma("pool", s0b, sgla[l, s, h], [], [("s0b", b2)], ("s0b", b2))
                    dma("pool", s0f, sgla[l, s, h], [], [("s0f", b2)], ("s0f", b2))
                    mm(PS[6][:, 0:256], Zv[:, s, :], s0b, False, s == 15, ["Z", ("s0b", b2)], [("ps", 6)])
                    pbank = 7 if s % 2 == 0 else 3
                    pss = PS[pbank][:, 0:256]
                    mm(pss, KMv[:, s, :], V, True, True, ["KM", vk], [("ps", pbank)])
                    sn, snk = SNB[s % 3]
                    tt("dve", sn, s0f, pss, ALU.add, [("s0f", b2), ("ps", pbank)], [snk])
                    ts("dve", sn, sn, ELs[:, s:s + 1], ALU.mult, [snk, "els"], [snk])
                    dma("sp", sgs[l, s, h], sn, [snk], [], ("sgs", s % 3))
                epilogue(PS[6][:, 0:256], [("ps", 6)], GGB9, ["ggb9"], h, 8, 0)

                sm = ggt
                dma("sp", sm, cc_s_out[l][h][0:128, :], [("ccsout", l, h)], ["ggt"], "sm")
                ts("dve", sm, sm, FLAG[:, 0:1], ALU.mult, ["ggt", "FLAG"], ["ggt"])
                cp("act", smb, sm, ["ggt"], ["smb"])
                act(STAT[:, 15:16], STAT[:, 14:15], AF.Exp, ["bacc"], ["ebt"], scale=-1.0 / 16.0)
                sf = r1f(SN_OFF, 256)
                stt("dve", sf, sm, STAT[:, 15:16], SST[:, hb, :], ALU.mult, ALU.add, ["ggt", "ebt", ("S", hb)], ["sn"])
                dma("sp", sgp[l, h], sf, ["sn"], [], ("sgs", 0))
                pending = h
                if h == 2:
                    pass

            corr_front(H - 1, 0)
            for t in range(8):
                if t + 1 < 8:
                    corr_front(H - 1, t + 1)
                corr_back(H - 1, t)

            def wout_load(krow0, cb, buf):
                wv, wkeys, slot = buf
                dma("pool", wv, w_out[l, krow0:krow0 + 1024, cb * 512:(cb + 1) * 512].rearrange("(k p) n -> p k n", p=128),
                    [], wkeys, slot, nobar=True)

            def wout_pass(krow0, bufs, preloaded=0, after_cb0=None):
                for cb in range(4):
                    if cb == 1 and after_cb0 is not None:
                        after_cb0()
                    buf = bufs[cb] if len(bufs) == 4 else bufs[cb % len(bufs)]
                    wv, wkeys, slot = buf
                    if cb >= preloaded:
                        wout_load(krow0, cb, buf)
                    for t in range(NT):
                        pb = PS[t % 2]
                        for k in range(8):
                            mm(pb[:], MTv[:, k, t * 128:(t + 1) * 128], wv[:, k, :], k == 0, k == 7,
                               [("mT", k, t), wkeys[0]], [("ps", t % 2)])
                        tt("dve", Xv[:, t, cb * 512:(cb + 1) * 512], Xv[:, t, cb * 512:(cb + 1) * 512], pb[:], ALU.add,
                           [("x", t, cb), ("ps", t % 2)], [("x", t, cb)])

            def u_loads(cbs=(0, 1)):
                for cb in cbs:
                    dma("pool", wb_view(cb, 512), w_in[l, :, C_U + cb * 512:C_U + (cb + 1) * 512].rearrange("(k p) n -> p k n", p=128),
                        [], [("wb", cb), ("wb", 0, 0), ("wb", 0, 1)] if cb == 0 else [("wb", cb)], ("wb", cb), nobar=True)

            chk(3)
            S.barrier()
            WO_R1 = [(r1b(4096 + i * 2048, 2048).rearrange("p (k n) -> p k n", n=512), [("wo", i)], ("wo", i)) for i in range(2)]
            WO_H0 = (WB[0][:, 0:4096].rearrange("p (k n) -> p k n", n=512), [("wb", 0, 0), ("wb", 0), ("wb4", 0)], ("wb", 0, 0))
            wout_load(0, 1, WO_R1[0])
            wout_load(0, 2, WO_R1[1])
            u_loads((1,))
            U78_OFF = 0
            SPB_OFF = 2048
            E_OFF, TA_OFF, TB_OFF = 4096, 5504, 6912
            DB_OFF = 8320
            WP_OFF = 9472
            PRE_OFF = 10496
            SPB = R1[0:120, SPB_OFF:SPB_OFF + 2048].rearrange("p (a c) -> p a c", c=1024)
            dma("sp", SPB, spool[l].rearrange("s t c -> (s t) c").rearrange("(a p) c -> p a c", p=120), [], ["SPB"], "spb")
            PRE = R1[0:15, PRE_OFF:PRE_OFF + 1024]
            WPv = r1b(WP_OFF, 1024).rearrange("p (g c d) -> p g c d", g=4, c=2)
            for g in range(4):
                dma("pool", WPv[:, g, :, :], w_pool[l, g].rearrange("(c p) d -> p c d", p=128), [], ["WP"], "wp")
            Dv = r1b(DB_OFF, 1152).rearrange("p (c t) -> p c t", t=T)
            wout_pass(0, [WO_H0, WO_R1[0], WO_R1[1], WO_R1[0]], preloaded=3, after_cb0=lambda: u_loads((0,)))
            S.barrier()
            chk(4)
            U78_OFF = 0
            SPB_OFF = 2048
            for ti, t in enumerate((7, 8)):
                u78 = r1f(U78_OFF + ti * 1024, 1024)
                for cb in range(2):
                    wv = wb_view(cb, 512)
                    for k in range(KC):
                        mm(PS[cb][:], HTv[:, k, t * 128:(t + 1) * 128], wv[:, k, :], k == 0, k == KC - 1,
                           [("hT", t), ("wb", cb)], [("ps", cb)])
                    cp("act" if cb == 0 else "dve", u78[:, cb * 512:(cb + 1) * 512], PS[cb][:], [("ps", cb)], [("u78", ti, cb)])
            u7 = r1f(U78_OFF, 1024)
            u8 = r1f(U78_OFF + 1024, 1024)
            u7k = [("u78", 0, 0), ("u78", 0, 1)]
            u8k = [("u78", 1, 0), ("u78", 1, 1)]
            dma("sp", spp[l], u7[113:128, :], u7k, [], "spp")
            dma("sp", cc_u_in[l][0:16, :], u7[112:128, :], u7k, [("ccuin", l)], "ccuin")
            A("pool", lambda e, l=l: e.collective_compute("AllGather", ALU.bypass, replica_groups=GROUPS,
                                                           ins=[cc_u_in[l]], outs=[cc_u_out[l]]),
              reads=[("ccuin", l)], writes=[("ccuout", l)], dma=True, slot=("ccu", l), inc=1)
            dma("sp", sps[l, :, 0:7, :], spool[l, :, 8:15, :], [], [], "sps0")
            for tt_ in range(8):
                srcap = R1[tt_:128:8, U78_OFF + 1024:U78_OFF + 2048]
                dma("sp", sps[l, :, 7 + tt_, :], srcap, u8k, [], "sps1")

            dma("sp", PRE, cc_u_out[l][1:16, :], [("ccuout", l)], ["PRE"], "pre")
            EOFFS = [4096, 0]

            def pool_proj(g, cc):
                ch = 2 * g + cc
                cb, j = ch // 4, ch % 4
                wv = wb_view(cb, 512)
                E = r1f(EOFFS[cc], 1407)
                ek_ = ("E", cc)
                ekw = [ek_] + ([("u78", 0, 0), ("u78", 0, 1), ("u78", 1, 0), ("u78", 1, 1)] if cc == 1 else [])
                Ep = E[:, 0:1039]
                Es = E[:, 1039:1407].rearrange("p (s c) -> p s c", c=23)
                for tb, (c0, cn) in enumerate(TB):
                    pb = PS[tb % 2]
                    for k in range(KC):
                        mm(pb[:, 0:cn], wv[:, k, j * 128:(j + 1) * 128], HTv[:, k, c0:c0 + cn], k == 0, k == KC - 1,
                           htk(tb_tiles(tb)) + [("wb", cb)], [("ps", tb % 2)])
                    if tb < 2:
                        cp("act", Ep[:, 15 + c0:15 + c0 + cn], pb[:, 0:cn], [("ps", tb % 2)], ekw)
                    else:
                        cp("act", Ep[:, 15 + c0:15 + 1024], pb[:, 0:256], [("ps", tb % 2)], [ek_])
                        cp("act", Es[:, :, 15:23], pb[:, 256:384].rearrange("p (s c) -> p s c", c=8), [("ps", tb % 2)], [ek_])

            def pool_windows(g, cc):
                w = 2 << g
                E = r1f(EOFFS[cc], 1407)
                ek_ = ("E", cc)
                Ep = E[:, 0:1039]
                Es = E[:, 1039:1407].rearrange("p (s c) -> p s c", c=23)
                ch = 2 * g + cc
                mm(PS[2][:, 0:15], PRE[:, ch * 128:(ch + 1) * 128], IDF[0:15, 0:15], True, True, ["PRE", "IDF"], [("ps", 2)])
                ts("dve", Ep[:, 0:15], PS[2][:, 0:15], FLAG[:, 0:1], ALU.mult, [("ps", 2), "FLAG"], [ek_])
                for a_ in range(2):
                    mm(PS[3][:, a_ * 120:(a_ + 1) * 120], SPB[:, a_, ch * 128:(ch + 1) * 128], IDF[0:120, 0:120], True, True,
                       ["SPB", "IDF"], [("ps", 3)])
                cp("dve", Es[:, :, 0:15], PS[3][:, 0:240].rearrange("p (s c) -> p s c", c=15), [("ps", 3)], [ek_])
                cur_p, cur_s, curk = Ep, Es, ek_
                bufs = [(TA_OFF, "TA"), (TB_OFF, "TB")]
                sh = 1
                step = 0
                while sh < w:
                    off, nk_ = bufs[step % 2]
                    Nt = r1f(off, 1407)
                    Np = Nt[:, 0:1039]
                    Ns = Nt[:, 1039:1407].rearrange("p (s c) -> p s c", c=23)
                    tt("pool", Np[:, sh:1039], cur_p[:, sh:1039], cur_p[:, 0:1039 - sh], ALU.add, [curk], [nk_])
                    tt("pool", Ns[:, :, sh:23], cur_s[:, :, sh:23], cur_s[:, :, 0:23 - sh], ALU.add, [curk], [nk_])
                    cur_p, cur_s, curk = Np, Ns, nk_
                    sh *= 2
                    step += 1
                stt("dve", Dv[:, cc, 0:1024], cur_p[:, 15:1039], 1.0 / w, Ep[:, 15:1039], ALU.mult, ALU.subtract,
                    [curk, ek_], [("D", cc)])
                stt("dve", Dv[:, cc, 1024:1152].rearrange("p (s c) -> p s c", c=8), cur_s[:, :, 15:23], 1.0 / w,
                    Es[:, :, 15:23], ALU.mult, ALU.subtract, [curk, ek_], [("D", cc)])
                t15 = STAT[:, 32:48]
                tt("dve", t15[:, 0:15], cur_p[:, 15:30], INVC[:, g * 15:(g + 1) * 15], ALU.mult, [curk, "INVC"], ["t15"])
                tt("dve", Dv[:, cc, 0:15], t15[:, 0:15], Ep[:, 15:30], ALU.subtract, ["t15", ek_, ("D", cc)], [("D", cc)])

            def pool_mm(g):
                for dc in range(2):
                    for tb, (c0, cn) in enumerate(TB):
                        pb = PS[4 + tb % 2]
                        for cc in range(2):
                            mm(pb[:, 0:cn], WPv[:, g, cc, dc * 128:(dc + 1) * 128], Dv[:, cc, c0:c0 + cn], cc == 0, cc == 1,
                               ["WP", ("D", cc)], [("ps", 4 + tb % 2)])
                        ts("dve", MTv[:, 2 * g + dc, c0:c0 + cn], pb[:, 0:cn], PSC[:, 2 * g + dc:2 * g + dc + 1], ALU.mult,
                           [("ps", 4 + tb % 2), "PSC"], [("mT", 2 * g + dc, t) for t in tb_tiles(tb)])

            WO_WB = [(WB[0][:, i * 4096:(i + 1) * 4096].rearrange("p (k n) -> p k n", n=512),
                      [("wb", 0, i), ("wb", 0), ("wb4", i)], ("wb", 0, i)) for i in range(2)]
            for g in range(4):
                pool_proj(g, 0)
                pool_proj(g, 1)
                if g == 1:
                    wout_load(1024, 0, WO_WB[0])
                    wout_load(1024, 1, WO_WB[1])
                if g > 0:
                    pool_mm(g - 1)
                pool_windows(g, 0)
                pool_windows(g, 1)
            pool_mm(3)
            S.barrier()
            ffn_loads(l, 0, alias=True, part=0)
            wout_pass(1024, WO_WB, preloaded=2)
            chk(5)

            ffn_loads(l, 0, alias=True, part=1)
            emit_norm(norm_ffn[l:l + 1, :])
            SG_OFF = [8192, 8576]
            ACTv = MT[:].rearrange("p (b f t) -> p b f t", b=2, f=FG)

            def down(gi, tile_done=None):
                b = gi % 2
                wd = r1b(WD_OFF[b], 4096).rearrange("p (f n) -> p f n", n=D)
                for t in range(NT):
                    if tile_done is not None and t > 0:
                        tile_done(t - 1)
                    for cb in range(4):
                        pb = PS[6 + (t * 4 + cb) % 2]
                        for f in range(FG):
                            mm(pb[:], ACTv[:, b, f, t * 128:(t + 1) * 128], wd[:, f, cb * 512:(cb + 1) * 512], f == 0, f == FG - 1,
                               [("act", b, f), ("wd", b)], [("ps", 6 + (t * 4 + cb) % 2)])
                        tt("dve", Xv[:, t, cb * 512:(cb + 1) * 512], Xv[:, t, cb * 512:(cb + 1) * 512], pb[:], ALU.add,
                           [("x", t, cb), ("ps", 6 + (t * 4 + cb) % 2)], [("x", t, cb)])

            for gi in range(NG):
                b = gi % 2
                if gi > 0:
                    ffn_loads(l, gi)
                for f in range(FG):
                    ff = gi * FG + f
                    slot = ffn_slot(ff)
                    wv = gu_view(slot)
                    for tb, (c0, cn) in enumerate(TB):
                        pg, pu = PS[tb], PS[3 + tb]
                        for k in range(KC):
                            mm(pg[:, 0:cn], wv[:, k, 0:128], HTv[:, k, c0:c0 + cn], k == 0, k == KC - 1,
                               htk(tb_tiles(tb)) + [("wb4", slot)], [("ps", tb)])
                        for k in range(KC):
                            mm(pu[:, 0:cn], wv[:, k, 128:256], HTv[:, k, c0:c0 + cn], k == 0, k == KC - 1,
                               htk(tb_tiles(tb)) + [("wb4", slot)], [("ps", 3 + tb)])
                        sg = r1f(SG_OFF[tb % 2], 384)
                        act(sg, pg[:, 0:cn], AF.Silu, [("ps", tb)], [("sg", tb % 2)])
                        tt("dve", ACTv[:, b, f, c0:c0 + cn], sg, pu[:, 0:cn], ALU.mult, [("sg", tb % 2), ("ps", 3 + tb)], [("act", b, f)])
                if gi > 0:
                    down(gi - 1)
            if l + 1 < depth:
                layer_prefetch(l + 1)
                down(NG - 1)
            else:
                GBF = r1f(4096, 2048)
                dma("sp", GBF, norm_final[0, :].partition_broadcast(128), [], ["gbf", ("wd", 1)], "gbc")

                def fin(t):
                    p = t % 2
                    junk = r1b(10240, 1024)
                    act(junk, Xv[:, t, :], AF.Square, xk(t), ["junk", ("ss", p)], scale=1.0 / np.sqrt(D), accum=STAT[:, p:p + 1])
                    act(STAT[:, 2 + p:3 + p], STAT[:, p:p + 1], AF.Ln, [("ss", p)], [("lnv", p)], bias=EPS)
                    act(STAT[:, 4 + p:5 + p], STAT[:, 2 + p:3 + p], AF.Exp, [("lnv", p)], [("rs", p)], scale=-0.5)
                    yt = r1f(6144 + p * D, D)
                    act(yt, Xv[:, t, :], AF.Copy, xk(t) + [("rs", p)], [("yt", p), ("wd", 1), ("sg", 0), ("sg", 1)],
                        scale=STAT[:, 4 + p:5 + p])
                    tt("pool", yt, yt, GBF, ALU.mult, [("yt", p), "gbf"], [("yt", p)])
                    dma("sp", y[t * 128:(t + 1) * 128, :], yt, [("yt", p)], [], ("yout", p))

                down(NG - 1, tile_done=fin)
                fin(NT - 1)
                final_done = True
            chk(6 + l)
        except _Stop:
            pass

        if not final_done:
            emit_norm(norm_final, to_y=True)
        info = S.emit()
    return nc, info


_CACHE = {}


def _consts(core):
    hf = core % 2
    tri = np.zeros((128, 256), np.float32)
    j = np.arange(128)[:, None]
    i = np.arange(128)[None, :]
    tri[:, 0:128] = (j <= i)
    tri[:, 128:256] = (j <= i) & ((j // 8) == (i // 8))
    seqsel = np.zeros((128, 32), np.float32)
    seqsel[:, 0] = 1.0
    for s in range(16):
        seqsel[8 * s:8 * s + 8, 16 + s] = 1.0
    rowmask = np.zeros((128, 16), np.float32)
    for s in range(16):
        rowmask[8 * s:8 * s + 8, s] = 1.0
    flag = np.full((128, 1), float(hf), np.float32)
    invc = np.zeros((128, 60), np.float32)
    for g, w in enumerate((2, 4, 8, 16)):
        for t in range(15):
            cnt = w if hf == 1 else min(w, t + 1)
            invc[:, g * 15 + t] = 1.0 / cnt
    return dict(c_identb=np.eye(128, dtype=np.float32).astype(ml_dtypes.bfloat16), c_identf=np.eye(128, dtype=np.float32),
                c_tri=tri, c_seqsel=seqsel, c_rowmask=rowmask, c_flag=flag, c_invc=invc)


def kernel(x_prompt, x_sample, state_gla, state_pool, norm_mix, w_in, w_gk_up, b_gk, gla_norm, w_pool,
           pool_scale, w_out, norm_ffn, w_gate, w_up, w_down, norm_final):
    f = lambda a: np.ascontiguousarray(np.asarray(a, dtype=np.float32))
    x_prompt, x_sample, state_gla, state_pool = f(x_prompt), f(x_sample), f(state_gla), f(state_pool)
    shared = dict(norm_mix=f(norm_mix), w_in=f(w_in), w_gk_up=f(w_gk_up), b_gk=f(b_gk), gla_norm=f(gla_norm),
                  w_pool=f(w_pool), pool_scale=f(pool_scale), w_out=f(w_out), norm_ffn=f(norm_ffn), w_gate=f(w_gate),
                  w_up=f(w_up), w_down=f(w_down), norm_final=f(norm_final).reshape(1, D))
    if "nc" not in _CACHE:
        import os
        _st = int(os.environ.get("MK_STAGE", "99"))
        _CACHE["nc"] = build_program(stage=_st)
        _CACHE["stage"] = _st
    nc, info = _CACHE["nc"]
    if _CACHE.get("stage", 99) < 6:
        for k_ in ("w_gate", "w_up", "w_down"):
            shared.pop(k_)
    in_maps = []
    for c in range(NCORES):
        b, hf = c // 2, c % 2
        xin = np.concatenate([x_prompt[b, hf * 1024:(hf + 1) * 1024], x_sample[16 * c:16 * c + 16].reshape(128, D)], 0)
        m = dict(shared)
        m["xin"] = np.ascontiguousarray(xin)
        m["sgla"] = np.ascontiguousarray(state_gla[:, 16 * c:16 * c + 16])
        m["spool"] = np.ascontiguousarray(state_pool[:, 16 * c:16 * c + 16])
        m.update(_consts(c))
        in_maps.append(m)
    res = run_bass_kernel_spmd(nc, in_maps, core_ids=list(range(NCORES)))
    y_prompt = np.zeros((4, 2048, D), np.float32)
    y_sample = np.zeros((128, 8, D), np.float32)
    sg_p = np.zeros((2, 4, H, DK, DV), np.float32)
    sp_p = np.zeros((2, 4, 15, 1024), np.float32)
    sg_s = np.zeros((2, 128, H, DK, DV), np.float32)
    sp_s = np.zeros((2, 128, 15, 1024), np.float32)
    for c in range(NCORES):
        r = res.results[c]
        b, hf = c // 2, c % 2
        y_prompt[b, hf * 1024:(hf + 1) * 1024] = r["y"][0:1024]
        y_sample[16 * c:16 * c + 16] = r["y"][1024:].reshape(16, 8, D)
        if hf == 1:
            sg_p[:, b] = r["sgp"]
            sp_p[:, b] = r["spp"]
        sg_s[:, 16 * c:16 * c + 16] = r["sgs"]
        sp_s[:, 16 * c:16 * c + 16] = r["sps"]
    return (y_prompt, y_sample, sg_p, sp_p, sg_s, sp_s)
```

```python
import contextlib
import numpy as np
import ml_dtypes
import concourse.bass as bass
import concourse.mybir as mybir
from concourse.bass_utils import run_bass_kernel_spmd

F32 = mybir.dt.float32
BF16 = mybir.dt.bfloat16
AF = mybir.ActivationFunctionType
ALU = mybir.AluOpType

NCORES = 8
T = 1152
NT = 9
D = 2048
KC = 16
DFF = 5632
NFF = 44
FG = 4
NG = NFF // FG
H = 4
DK = 128
DV = 256
EPS = 1e-6
C_Q, C_K, C_V, C_G, C_GK, C_U = 0, 512, 1024, 2048, 3072, 3088
INW = 4112


class _Op:
    __slots__ = ("eng", "fn", "deps", "dma", "slot", "sig", "sigval", "idx", "inc")


class Sched:
    def __init__(self, nc):
        self.nc = nc
        self.ops = []
        self.lw = {}
        self.rd = {}
        self.bar = {}
        self.last_on = {}
        self.dma_since_bar = []

    def barrier(self):
        deps = set(self.last_on.values()) | set(self.dma_since_bar)
        for e in ("pe", "act", "dve", "pool", "sp"):
            self.bar[e] = set(deps) | self.bar.get(e, set())
        self.dma_since_bar = []

    def add(self, eng, fn, reads=(), writes=(), dma=False, slot=None, inc=16, nobar=False):
        op = _Op()
        op.eng, op.fn, op.dma, op.slot = eng, fn, dma, slot
        op.inc = inc
        op.idx = len(self.ops)
        op.sig = False
        op.sigval = 0
        deps = set()
        ops = self.ops
        for r in reads:
            w = self.lw.get(r)
            if w is not None:
                deps.add(w)
        for wkey in writes:
            w = self.lw.get(wkey)
            if w is not None:
                o = ops[w]
                if o.dma and dma:
                    pass
                elif o.dma or dma or o.eng != eng or eng != "pe":
                    deps.add(w)
            for x in self.rd.get(wkey, ()):
                o = ops[x]
                if o.dma or dma or o.eng != eng or eng != "pe":
                    deps.add(x)
        b = self.bar.pop(eng, None)
        if b:
            deps |= b
        deps.discard(op.idx)
        red = {}
        for d_ in deps:
            o = ops[d_]
            key = ("s", o.slot) if o.dma else ("e", o.eng)
            if key not in red or red[key] < d_:
                red[key] = d_
        op.deps = set(red.values())
        ops.append(op)
        for r in reads:
            self.rd.setdefault(r, []).append(op.idx)
        for wkey in writes:
            self.lw[wkey] = op.idx
            self.rd[wkey] = []
        if dma:
            if not nobar:
                self.dma_since_bar.append(op.idx)
        else:
            self.last_on[eng] = op.idx
        return op.idx

    def emit(self):
        nc = self.nc
        ops = self.ops
        for op in ops:
            for d in op.deps:
                ops[d].sig = True
        engs = ["pe", "act", "dve", "pool", "sp"]
        cnt = {e: 0 for e in engs}
        slotcnt = {}
        for op in ops:
            if op.dma:
                slotcnt[op.slot] = slotcnt.get(op.slot, 0) + op.inc
                op.sigval = slotcnt[op.slot]
            elif op.sig:
                cnt[op.eng] += 1
                op.sigval = cnt[op.eng]
        slots = sorted(slotcnt.keys(), key=str)
        with contextlib.ExitStack() as es:
            esem = {e: es.enter_context(nc.semaphore("se_" + e)) for e in engs}
            ssem = {s: es.enter_context(nc.semaphore("sd_%d" % i)) for i, s in enumerate(slots)}
            block = es.enter_context(nc.Block())
            per_eng = {e: [op for op in ops if op.eng == e] for e in engs}

            def run(e, engobj):
                waited = {}
                for op in per_eng[e]:
                    need = {}
                    for d in op.deps:
                        o = ops[d]
                        if o.dma:
                            key = ("s", o.slot)
                            sem = ssem[o.slot]
                        else:
                            if o.eng == e and not op.dma and False:
                                continue
                            key = ("e", o.eng)
                            sem = esem[o.eng]
                        v = o.sigval
                        if waited.get(key, 0) >= v:
                            continue
                        if key not in need or need[key][1] < v:
                            need[key] = (sem, v)
                    for key, (sem, v) in need.items():
                        engobj.wait_ge(sem, v)
                        waited[key] = v
                    ins = op.fn(engobj)
                    if op.dma:
                        ins.then_inc(ssem[op.slot], op.inc)
                    elif op.sig:
                        ins.then_inc(esem[op.eng], 1)
                if e == "sp":
                    for s in slots:
                        if waited.get(("s", s), 0) < slotcnt[s]:
                            engobj.wait_ge(ssem[s], slotcnt[s])

            @block.tensor
            def _(eng):
                run("pe", eng)

            @block.scalar
            def _(eng):
                run("act", eng)

            @block.vector
            def _(eng):
                run("dve", eng)

            @block.gpsimd
            def _(eng):
                run("pool", eng)

            @block.sync
            def _(eng):
                run("sp", eng)
        return dict(n_ops=len(ops), n_sig=dict(cnt), n_slots=len(slots))


class _Stop(Exception):
    pass


def build_program(depth=2, stage=99):
    nc = bass.Bass("TRN2", target_bir_lowering=False)
    di = lambda n, s, dt=F32: nc.dram_tensor(n, s, dt, kind="ExternalInput").ap()
    do = lambda n, s, dt=F32: nc.dram_tensor(n, s, dt, kind="ExternalOutput").ap()
    xin = di("xin", [T, D])
    sgla = di("sgla", [2, 16, H, DK, DV])
    spool = di("spool", [2, 16, 15, 1024])
    norm_mix = di("norm_mix", [2, D])
    w_in = di("w_in", [2, D, INW])
    w_gk_up = di("w_gk_up", [2, 16, 512])
    b_gk = di("b_gk", [2, 512])
    gla_norm = di("gla_norm", [2, DV])
    w_pool = di("w_pool", [2, 4, 256, 256])
    pool_scale = di("pool_scale", [2, 1024])
    w_out = di("w_out", [2, D, D])
    norm_ffn = di("norm_ffn", [2, D])
    if stage >= 6:
        w_gate = di("w_gate", [2, D, DFF])
        w_up = di("w_up", [2, D, DFF])
        w_down = di("w_down", [2, DFF, D])
    norm_final = di("norm_final", [1, D])
    c_identb = di("c_identb", [128, 128], BF16)
    c_identf = di("c_identf", [128, 128])
    c_tri = di("c_tri", [128, 256])
    c_seqsel = di("c_seqsel", [128, 32])
    c_rowmask = di("c_rowmask", [128, 16])
    c_flag = di("c_flag", [128, 1])
    c_invc = di("c_invc", [128, 60])
    y = do("y", [T, D])
    sgp = do("sgp", [2, H, DK, DV])
    spp = do("spp", [2, 15, 1024])
    sgs = do("sgs", [2, 16, H, DK, DV])
    sps = do("sps", [2, 16, 15, 1024])
    cc_u_in = [nc.dram_tensor("cc_u_in%d" % l, [16, 1024], F32).ap() for l in range(2)]
    cc_u_out = [nc.dram_tensor("cc_u_out%d" % l, [32, 1024], F32).ap() for l in range(2)]
    cc_s_in = [[nc.dram_tensor("cc_s_in%d_%d" % (l, h), [128, 256], F32).ap() for h in range(H)] for l in range(2)]
    cc_s_out = [[nc.dram_tensor("cc_s_out%d_%d" % (l, h), [256, 256], F32).ap() for h in range(H)] for l in range(2)]
    GROUPS = [[2 * i, 2 * i + 1] for i in range(NCORES // 2)]

    es = contextlib.ExitStack()
    with es:
        sb = lambda name, shape, dt: es.enter_context(nc.sbuf_tensor(name, shape, dt))
        X = sb("X", [128, NT * D], F32)
        HT = sb("HT", [128, KC * T], BF16)
        MT = sb("MT", [128, 8 * T], BF16)
        WB = [sb("WB0", [128, KC * 512], BF16), sb("WB1", [128, KC * 512], BF16)]
        IDB = sb("IDB", [128, 128], BF16)
        IDF = sb("IDF", [128, 128], F32)
        TRI = sb("TRI", [128, 256], F32)
        SEQSEL = sb("SEQSEL", [128, 32], F32)
        ROWMASK = sb("ROWMASK", [128, 16], F32)
        FLAG = sb("FLAG", [128, 1], F32)
        INVC = sb("INVC", [128, 60], F32)
        PSC = sb("PSC", [128, 8], F32)
        GNB = sb("GNB", [128, 256], F32)
        WGKLR = sb("WGKLR", [128, KC * 16], BF16)
        STAT = sb("STAT", [128, 48], F32)
        R1W = 11736
        R1 = sb("R1", [128, R1W], F32)
        PS = [es.enter_context(nc.psum_tensor("PS%d" % i, [128, 512], F32)) for i in range(8)]

        def r1f(off, n, parts=128):
            return R1[0:parts, off:off + n]

        def r1b(off, n, parts=128):
            return R1[0:parts, off:off + n].bitcast(BF16)

        Xv = X[:].rearrange("p (t d) -> p t d", d=D)
        HTv = HT[:].rearrange("p (k t) -> p k t", t=T)
        MTv = MT[:].rearrange("p (k t) -> p k t", t=T)
        GBC = MT[:, 0:2 * D].bitcast(F32)

        S = Sched(nc)
        A = S.add

        def dma(q, out, in_, reads, writes, slot, slow=False, nobar=False):
            if slow:
                A(q, lambda e: e.dma_start(out=out, in_=in_, allow_slow_non_contiguous=True), reads=reads, writes=writes, dma=True, slot=slot, nobar=nobar)
            else:
                A(q, lambda e: e.dma_start(out=out, in_=in_), reads=reads, writes=writes, dma=True, slot=slot, nobar=nobar)

        def mm(out, lhsT, rhs, start, stop, reads, writes):
            A("pe", lambda e: e.matmul(out, lhsT=lhsT, rhs=rhs, start=start, stop=stop), reads=reads, writes=writes)

        def tr(out, in_, reads, writes):
            A("pe", lambda e: e.transpose(out, in_, IDB[:]), reads=list(reads) + ["IDB"], writes=writes)

        def act(out, in_, func, reads, writes, scale=1.0, bias=0.0, accum=None):
            if accum is None:
                A("act", lambda e: e.activation(out=out, in_=in_, func=func, bias=bias, scale=scale), reads=reads, writes=writes)
            else:
                A("act", lambda e: e.activation(out=out, in_=in_, func=func, bias=bias, scale=scale, accum_out=accum), reads=reads, writes=writes)

        def tt(eng, out, in0, in1, op, reads, writes):
            A(eng, lambda e: e.tensor_tensor(out=out, in0=in0, in1=in1, op=op), reads=reads, writes=writes)

        def ts(eng, out, in0, s1, op0, reads, writes):
            if op0 == ALU.mult:
                A(eng, lambda e: e.tensor_scalar_mul(out, in0, s1), reads=reads, writes=writes)
            else:
                A(eng, lambda e: e.tensor_scalar_add(out, in0, s1), reads=reads, writes=writes)

        def stt(eng, out, in0, scalar, in1, op0, op1, reads, writes):
            A(eng, lambda e: e.scalar_tensor_tensor(out=out, in0=in0, scalar=scalar, in1=in1, op0=op0, op1=op1), reads=reads, writes=writes)

        def cp(eng, out, in_, reads, writes):
            if eng == "act":
                A("act", lambda e: e.copy(out=out, in_=in_), reads=reads, writes=writes)
            else:
                A(eng, lambda e: e.tensor_copy(out=out, in_=in_), reads=reads, writes=writes)

        def memset(eng, ap, val, writes):
            A(eng, lambda e: e.memset(ap, val), writes=writes)

        xk = lambda t: [("x", t, cb) for cb in range(4)]
        htk = lambda tiles: [("hT", t) for t in tiles]
        TB = [(0, 384), (384, 384), (768, 384)]
        tb_tiles = lambda tb: [3 * tb, 3 * tb + 1, 3 * tb + 2]

        dma("sp", IDB[:], c_identb, [], ["IDB"], "c0")
        dma("sp", IDF[:], c_identf, [], ["IDF"], "c1")
        dma("sp", TRI[:], c_tri, [], ["TRI"], "c2")
        dma("sp", SEQSEL[:], c_seqsel, [], ["SEQSEL"], "c3")
        dma("sp", ROWMASK[:], c_rowmask, [], ["ROWMASK"], "c4")
        dma("sp", FLAG[:], c_flag, [], ["FLAG"], "c5")
        dma("sp", INVC[:], c_invc, [], ["INVC"], "c6")
        dma("sp", GBC, norm_mix[0, :].partition_broadcast(128), [], ["gbc"], "gbc")
        for t in range(NT):
            dma("sp", Xv[:, t, :], xin[t * 128:(t + 1) * 128, :], [], xk(t), ("xin", t), nobar=True)

        XN_OFF, JUNK_OFF, YT_OFF = 0, 1024, 2048

        def emit_norm(gvec, to_y=False, skip_gbc=False):
            S.barrier()
            if not skip_gbc:
                dma("sp", GBC, gvec[0, :].partition_broadcast(128), [], ["gbc"], "gbc")
            XNb = [r1b(8192, 1024), r1b(9216, 1024)]
            junk = r1b(10240, 1024)

            def front(t):
                p = t % 2
                act(junk, Xv[:, t, :], AF.Square, xk(t), ["junk", ("ss", p)], scale=1.0 / np.sqrt(D), accum=STAT[:, p:p + 1])
                act(STAT[:, 2 + p:3 + p], STAT[:, p:p + 1], AF.Ln, [("ss", p)], [("lnv", p)], bias=EPS)
                act(STAT[:, 4 + p:5 + p], STAT[:, 2 + p:3 + p], AF.Exp, [("lnv", p)], [("rs", p)], scale=-0.5)
                rs = STAT[:, 4 + p:5 + p]
                if to_y:
                    yt = r1f(YT_OFF + p * D, D)
                    stt("dve", yt, Xv[:, t, :], rs, GBC, ALU.mult, ALU.mult, xk(t) + [("rs", p), "gbc"], [("yt", p)])
                    dma("sp", y[t * 128:(t + 1) * 128, :], yt, [("yt", p)], [], ("yout", p))
                else:
                    stt("dve", XNb[p], Xv[:, t, :], rs, GBC, ALU.mult, ALU.mult, xk(t) + [("rs", p), "gbc"], [("xn", p)])

            def back(t):
                p = t % 2
                xn = XNb[p]
                for half in range(2):
                    bank = 4 + 2 * p + half
                    pb = PS[bank][:].bitcast(BF16)
                    for j in range(8):
                        k = half * 8 + j
                        tr(pb[:, j * 128:(j + 1) * 128], xn[:, k * 128:(k + 1) * 128], [("xn", p)], [("ps", bank)])
                    dst = HTv[:, half * 8:(half + 1) * 8, t * 128:(t + 1) * 128]
                    src = pb.rearrange("p (k c) -> p k c", c=128)
                    cp("act" if half == 0 else "dve", dst, src, [("ps", bank)], [("hT", t)])

            front(0)
            for t in range(NT):
                if t + 1 < NT:
                    front(t + 1)
                if not to_y:
                    back(t)
            S.barrier()

        def load_w_cols(slot, dst_col0, src2d, ncols, nk=KC):
            pass

        def wb_view(slot, width, nk=KC):
            return WB[slot][:, 0:nk * width].rearrange("p (k n) -> p k n", n=width)

        def load_head_w(l, h, alias=False):
            hb = h % 2
            wa = WB[0][:, hb * 4096:(hb + 1) * 4096].rearrange("p (k n) -> p k n", n=256)
            wbv = wb_view(1, 512)
            wak = ("wb", 0, hb)
            ka = [wak] + ([("wb", 0), ("wb4", 0), ("wb4", 1)] if alias else [])
            kb = [("wb", 1)] + ([("wb4", 2), ("wb4", 3)] if alias else [])
            srcs = [(wa, 0, C_Q + h * 128, 128, ka, wak), (wa, 128, C_K + h * 128, 128, ka, wak),
                    (wbv, 0, C_V + h * 256, 256, kb, ("wb", 1)), (wbv, 256, C_G + h * 256, 256, kb, ("wb", 1))]
            for (dst, d0, c0, cn, wkeys, slot) in srcs:
                dma("pool", dst[:, :, d0:d0 + cn], w_in[l, :, c0:c0 + cn].rearrange("(k p) n -> p k n", p=128),
                    [], wkeys, slot, nobar=True)

        def layer_prefetch(l):
            dma("sp", PSC[:], pool_scale[l, :].rearrange("(k p) -> p k", p=128), [], ["PSC"], "psc", slow=True)
            dma("sp", GNB[:], gla_norm[l, :].partition_broadcast(128), [], ["GNB"], "gnb")
            dma("pool", WGKLR[:].rearrange("p (k n) -> p k n", n=16),
                w_in[l, :, C_GK:C_GK + 16].rearrange("(k p) n -> p k n", p=128), [], ["WGKLR"], "wgklr")
            load_head_w(l, 0, alias=True)

        WD_OFF = [0, 4096]

        def gu_view(slot):
            return WB[slot // 2][:, (slot % 2) * 4096:(slot % 2 + 1) * 4096].rearrange("p (k n) -> p k n", n=256)

        def ffn_slot(ff):
            return (ff + 2) % 4

        def ffn_loads(l, gi, alias=False, part=None):
            b = gi % 2
            for f in range(FG):
                if part == 0 and f >= 2:
                    continue
                if part == 1 and f < 2:
                    continue
                ff = gi * FG + f
                slot = ffn_slot(ff)
                wv = gu_view(slot)
                wk = [("wb4", slot)] + (([("wb", 0), ("wb", 0, slot)] if slot < 2 else [("wb", 1)]) if alias else [])
                dma("pool", wv[:, :, 0:128], w_gate[l, :, ff * 128:(ff + 1) * 128].rearrange("(k p) n -> p k n", p=128),
                    [], wk, ("wb4", slot), nobar=True)
                dma("pool", wv[:, :, 128:256], w_up[l, :, ff * 128:(ff + 1) * 128].rearrange("(k p) n -> p k n", p=128),
                    [], wk, ("wb4", slot), nobar=True)
            if part == 0:
                return
            wd = r1b(WD_OFF[b], 4096).rearrange("p (f n) -> p f n", n=D)
            dma("pool", wd, w_down[l, gi * FG * 128:(gi + 1) * FG * 128, :].rearrange("(f p) n -> p f n", p=128),
                [], [("wd", b)], ("wd", b), nobar=True)

        def chk(n):
            if stage <= n:
                raise _Stop()

        layer_prefetch(0)
        final_done = False
        try:
          for l in range(depth):
            chk(0)
            emit_norm(norm_mix[l:l + 1, :], skip_gbc=(l == 0))
            chk(1)
            GKL_OFF = 0
            WGK_OFF = 576
            ORAW_OFF = 832
            GGB_OFF = 2880
            QS_OFF = 3904
            SST_OFF = 4416
            SBF_OFF = 4928
            S0F_OFF = 5184
            S0B_OFF = 5696
            Z_OFF = 5952
            KM_OFF = 7040
            V_OFF = 8064
            GGT_OFF = 8448
            SP_OFF = 8704
            EQ_OFF = 8832
            EK_OFF = 8960
            QKT_OFF = 9088
            QKTT_OFF = 9344
            SCM_OFF = 9600
            JK_OFF = 9664
            OG_OFF = 9792
            SN_OFF = 10048
            OF_OFF = 10304
            SM_OFF = 10816
            SMB_OFF = 11072
            assert SMB_OFF + 128 <= R1W

            GKL = r1b(GKL_OFF, 576, 17)
            WGK = r1b(WGK_OFF, 256, 17)
            WGKF = r1f(ORAW_OFF, 512, 17)
            memset("dve", GKL, 1.0, ["GKL"])
            dma("sp", WGKF[0:16, :], w_gk_up[l], [], ["WGKF"], "wgkf")
            dma("sp", WGKF[16:17, :], b_gk[l:l + 1, :], [], ["WGKF"], "wgkf")
            cp("dve", WGK, WGKF, ["WGKF"], ["WGK"])
            wlr = WGKLR[:].rearrange("p (k n) -> p k n", n=16)
            for tb, (c0, cn) in enumerate(TB):
                for k in range(KC):
                    mm(PS[2][0:16, 0:cn], wlr[:, k, :], HTv[:, k, c0:c0 + cn], k == 0, k == KC - 1,
                       htk(tb_tiles(tb)) + ["WGKLR"], [("ps", 2)])
                cp("act", GKL[0:16, c0:c0 + cn], PS[2][0:16, 0:cn], [("ps", 2)], ["GKL"])

            Zfull = r1b(Z_OFF, 1088)
            memset("pool", Zfull, 0.0, ["Z"])
            Zdiag = Zfull.rearrange("p (s c) -> p s c", c=136)[:, :, 0:8]
            Zv = Zfull[:, 0:2048].rearrange("p (s t) -> p s t", t=128)
            KMv = r1b(KM_OFF, 1024).rearrange("p (s d) -> p s d", d=128)
            S.barrier()
            SST = r1f(SST_OFF, 512).rearrange("p (h v) -> p h v", v=256)
            SBF = r1b(SBF_OFF, 256).rearrange("p (h v) -> p h v", v=256)
            ORAW = r1f(ORAW_OFF, 2048).rearrange("p (t v) -> p t v", v=256)
            GGB = r1b(GGB_OFF, 1024).rearrange("p (t v) -> p t v", v=256)
            QS = r1b(QS_OFF, 512)
            Vb = [r1b(V_OFF + i * 128, 128) for i in range(3)]
            QKT = [r1b(QKT_OFF + i * 128, 128) for i in range(2)]
            QKTT = [r1b(QKTT_OFF + i * 128, 128) for i in range(2)]
            OGb = [r1b(OG_OFF + i * 128, 128) for i in range(2)]
            OFb = [r1f(OF_OFF + i * 256, 256) for i in range(2)]
            ggt = r1f(GGT_OFF, 256)
            sp_ = r1f(SP_OFF, 128)
            eq = r1f(EQ_OFF, 128)
            ek = r1f(EK_OFF, 128)
            scm = r1b(SCM_OFF, 64)
            ELs = STAT[:, 16:32]
            V8 = r1b(JK_OFF, 128)
            GGB9 = r1b(SM_OFF, 128)
            scm8 = r1b(SM_OFF + 128, 64)

            def epi_front(osrc, okeys, gg, ggkeys, p):
                og = OGb[p]
                c = 8 + 3 * p
                act(og, osrc, AF.Square, okeys, [("og", p), ("e_ss", p)], scale=1.0 / 16.0, accum=STAT[:, c:c + 1])
                act(STAT[:, c + 1:c + 2], STAT[:, c:c + 1], AF.Ln, [("e_ss", p)], [("e_ln", p)], bias=EPS)
                act(STAT[:, c + 2:c + 3], STAT[:, c + 1:c + 2], AF.Exp, [("e_ln", p)], [("e_rs", p)], scale=-0.5)
                stt("dve", og, osrc, STAT[:, c + 2:c + 3], gg, ALU.mult, ALU.mult, okeys + [("e_rs", p)] + ggkeys, [("og", p)])

            def epi_back(h, t, p):
                og = OGb[p]
                pb = PS[5][:, 256 + 128 * p:384 + 128 * p].bitcast(BF16)
                for j in range(2):
                    tr(pb[:, j * 128:(j + 1) * 128], og[:, j * 128:(j + 1) * 128], [("og", p)], [("ps", 5)])
                cp("act", MTv[:, 2 * h:2 * h + 2, t * 128:(t + 1) * 128], pb.rearrange("p (k c) -> p k c", c=128),
                   [("ps", 5)], [("mT", 2 * h, t), ("mT", 2 * h + 1, t)])

            def epilogue(osrc, okeys, gg, ggkeys, h, t, p):
                epi_front(osrc, okeys, gg, ggkeys, p)
                epi_back(h, t, p)

            smb = r1b(SMB_OFF, 128)
            PC = [(6, PS[6][:, 256:512]), (7, PS[7][:, 256:512])]

            def corr_front(hp, t):
                bank, reg = PC[t % 2]
                mm(reg, QS[:, t * 128:(t + 1) * 128], smb, True, True, [("qs", t), "smb"], [("ps", bank)])
                of = OFb[t % 2]
                tt("dve", of, ORAW[:, t, :], reg, ALU.add, [("oraw", t), ("ps", bank)], [("of", t % 2)])
                epi_front(of, [("of", t % 2)], GGB[:, t, :], [("ggb", t)], t % 2)

            def corr_back(hp, t):
                epi_back(hp, t, t % 2)

            pending = None
            for h in range(H):
                hb = h % 2
                wa = WB[0][:, hb * 4096:(hb + 1) * 4096].rearrange("p (k n) -> p k n", n=256)
                wbv = wb_view(1, 512)
                wak = ("wb", 0, hb)
                if h == H - 1:
                    dma("pool", WB[0][:, 0:4096].rearrange("p (k n) -> p k n", n=512),
                        w_out[l, 0:1024, 0:512].rearrange("(k p) n -> p k n", p=128),
                        [], [("wb", 0, 0), ("wb", 0), ("wb4", 0)], ("wb", 0, 0), nobar=True)
                memset("dve", SST[:, hb, :], 0.0, [("S", hb)])
                memset("pool", SBF[:, hb, :], 0.0, [("Sb", hb)])
                memset("dve", STAT[:, 14:15], 0.0, ["bacc"])

                def P(t):
                    return 0 if t == 8 else t + 1

                def tinfo(t):
                    smp = (t == 8)
                    v_ = 1 if smp else 0
                    return smp, v_, (16 if smp else 1), TRI[:, v_ * 128:(v_ + 1) * 128]

                def S1(t):
                    smp, v_, nseq, tri = tinfo(t)
                    pq = PS[2 + P(t) % 2]
                    pvg = PS[P(t) % 2]
                    for k in range(KC):
                        mm(pq[:, 0:256], HTv[:, k, t * 128:(t + 1) * 128], wa[:, k, :], k == 0, k == KC - 1,
                           [("hT", t), wak], [("ps", 2 + P(t) % 2)])
                    mm(pq[:, 256:384], GKL[:, t * 128:(t + 1) * 128], WGK[:, h * 128:(h + 1) * 128], True, True,
                       ["GKL", "WGK"], [("ps", 2 + P(t) % 2)])
                    for k in range(KC):
                        mm(pvg[:], HTv[:, k, t * 128:(t + 1) * 128], wbv[:, k, :], k == 0, k == KC - 1,
                           [("hT", t), ("wb", 1)], [("ps", P(t) % 2)])
                    act(sp_, pq[:, 256:384], AF.Exp, [("ps", 2 + P(t) % 2)], ["sp"], scale=-1.0)
                    act(sp_, sp_, AF.Ln, ["sp"], ["sp"], bias=1.0)
                    V = V8 if smp else Vb[P(t) % 3]
                    cp("act", V, pvg[:, 0:256], [("ps", P(t) % 2)], ["V8" if smp else ("V", P(t) % 3)])
                    act(ggt, pvg[:, 256:512], AF.Exp, [("ps", P(t) % 2)], ["ggt"], scale=-1.0)
                    ts("dve", ggt, ggt, 1.0, ALU.add, ["ggt"], ["ggt"])
                    A("dve", lambda e: e.reciprocal(out=ggt, in_=ggt), reads=["ggt"], writes=["ggt"])
                    tt("dve", ggt, ggt, pvg[:, 256:512], ALU.mult, ["ggt", ("ps", P(t) % 2)], ["ggt"])
                    if smp:
                        tt("dve", GGB9, ggt, GNB[:], ALU.mult, ["ggt", "GNB"], ["ggb9"])
                    else:
                        tt("dve", GGB[:, t, :], ggt, GNB[:], ALU.mult, ["ggt", "GNB"], [("ggb", t)])

                def S2(t):
                    smp, v_, nseq, tri = tinfo(t)
                    pq = PS[2 + P(t) % 2]
                    mm(PS[4][:, 0:128], tri, sp_, True, True, ["TRI", "sp"], [("ps", 4)])
                    sel = SEQSEL[:, v_ * 16:v_ * 16 + nseq]
                    mm(PS[4][:, 128:128 + nseq], sp_, sel, True, True, ["sp", "SEQSEL"], [("ps", 4)])
                    act(eq, PS[4][:, 0:128], AF.Exp, [("ps", 4)], ["eq"], scale=-1.0 / 16.0)
                    act(ek, PS[4][:, 0:128], AF.Exp, [("ps", 4)], ["ek"], scale=1.0 / 16.0)
                    if smp:
                        act(ELs, PS[4][:, 128:144], AF.Exp, [("ps", 4)], ["els"], scale=-1.0 / 16.0)
                    else:
                        act(STAT[:, 6 + P(t) % 2:7 + P(t) % 2], PS[4][:, 128:129], AF.Exp, [("ps", 4)], [("elp", P(t) % 2)], scale=-1.0 / 16.0)
                        act(STAT[:, 32 + P(t) % 2:33 + P(t) % 2], STAT[:, 14:15], AF.Exp, ["bacc"], [("eb", P(t) % 2)], scale=-1.0 / 16.0)
                        tt("dve", STAT[:, 14:15], STAT[:, 14:15], PS[4][:, 128:129], ALU.add, ["bacc", ("ps", 4)], ["bacc"])
                    qkt = QKT[P(t) % 2]
                    stt("dve", qkt[:, 0:128], pq[:, 0:128], float(DK ** -0.5), eq, ALU.mult, ALU.mult,
                        [("ps", 2 + P(t) % 2), "eq"], [("qkt", P(t) % 2)])
                    tt("dve", qkt[:, 128:256], pq[:, 128:256], ek, ALU.mult, [("ps", 2 + P(t) % 2), "ek"], [("qkt", P(t) % 2)])

                def S3(t):
                    smp, v_, nseq, tri = tinfo(t)
                    qkt = QKT[P(t) % 2]
                    qktt = QKTT[P(t) % 2]
                    pbt = PS[5][:, 0:128].bitcast(BF16)
                    tr(pbt[:, 0:128], qkt[:, 0:128], [("qkt", P(t) % 2)], [("ps", 5)])
                    tr(pbt[:, 128:256], qkt[:, 128:256], [("qkt", P(t) % 2)], [("ps", 5)])
                    cp("act", qktt, pbt, [("ps", 5)], [("qktt", P(t) % 2)])
                    if not smp:
                        act(QS[:, t * 128:(t + 1) * 128], pbt[:, 0:128], AF.Copy, [("ps", 5), ("eb", P(t) % 2)], [("qs", t)],
                            scale=STAT[:, 32 + P(t) % 2:33 + P(t) % 2])

                def S4(t):
                    smp, v_, nseq, tri = tinfo(t)
                    qktt = QKTT[P(t) % 2]
                    mm(PS[5][:, 128:256], qktt[:, 128:256], qktt[:, 0:128], True, True, [("qktt", P(t) % 2)], [("ps", 5)])
                    if smp:
                        tt("dve", scm8, PS[5][:, 128:256], tri, ALU.mult, [("ps", 5), "TRI"], ["scm8"])
                    else:
                        tt("dve", scm, PS[5][:, 128:256], tri, ALU.mult, [("ps", 5), "TRI"], ["scm"])

                def S5(t):
                    qkt = QKT[P(t) % 2]
                    qktt = QKTT[P(t) % 2]
                    V = Vb[P(t) % 3]
                    vk = ("V", P(t) % 3)
                    mm(PS[6][:, 0:256], scm, V, True, False, ["scm", vk], [("ps", 6)])
                    mm(PS[6][:, 0:256], qktt[:, 0:128], SBF[:, hb, :], False, True, [("qktt", P(t) % 2), ("Sb", hb)], [("ps", 6)])
                    cp("act", ORAW[:, t, :], PS[6][:, 0:256], [("ps", 6)], [("oraw", t)])
                    mm(PS[7][:, 0:256], qkt[:, 128:256], V, True, True, [("qkt", P(t) % 2), vk], [("ps", 7)])
                    tt("dve", SST[:, hb, :], SST[:, hb, :], PS[7][:, 0:256], ALU.add, [("S", hb), ("ps", 7)], [("S", hb)])
                    ts("dve", SST[:, hb, :], SST[:, hb, :], STAT[:, 6 + P(t) % 2:7 + P(t) % 2], ALU.mult, [("S", hb), ("elp", P(t) % 2)], [("S", hb)])
                    cp("pool", SBF[:, hb, :], SST[:, hb, :], [("S", hb)], [("Sb", hb)])

                def S5z(t):
                    qkt = QKT[P(t) % 2]
                    qktt = QKTT[P(t) % 2]
                    cp("pool", Zdiag, qktt[:, 0:128].rearrange("p (s j) -> p s j", j=8), [("qktt", P(t) % 2)], ["Z"])
                    A("dve", lambda e, qkt=qkt: e.tensor_tensor(
                        out=KMv, in0=qkt[:, 128:256].unsqueeze(1).to_broadcast([128, 16, 128]),
                        in1=ROWMASK[:].unsqueeze(2).to_broadcast([128, 16, 128]), op=ALU.mult),
                      reads=[("qkt", P(t) % 2), "ROWMASK"], writes=["KM"])

                tl = [8] + list(range(8))
                NP_ = len(tl)
                for i in range(NP_ + 2):
                    if pending is not None and i < 8:
                        corr_front(pending, i)
                    if 0 <= i - 1 < NP_:
                        S2(tl[i - 1])
                    if i < NP_:
                        S1(tl[i])
                    if i == NP_ - 1 and h + 1 < H:
                        load_head_w(l, h + 1)
                    if pending is not None and i < 8:
                        corr_back(pending, i)
                    if 0 <= i - 1 < NP_:
                        S3(tl[i - 1])
                    if 0 <= i - 2 < NP_:
                        if tl[i - 2] == 8:
                            S5z(8)
                        else:
                            S5(tl[i - 2])
                    if 0 <= i - 1 < NP_:
                        S4(tl[i - 1])

                dma("sp", cc_s_in[l][h], SST[:, hb, :], [("S", hb)], [("ccsin", l, h)], "ccsin")
                A("pool", lambda e, l=l, h=h: e.collective_compute("AllGather", ALU.bypass, replica_groups=GROUPS,
                                                                    ins=[cc_s_in[l][h]], outs=[cc_s_out[l][h]]),
                  reads=[("ccsin", l, h)], writes=[("ccsout", l, h)], dma=True, slot=("ccs", l, h), inc=1)

                t = 8
                V = V8
                vk = "V8"
                mm(PS[6][:, 0:256], scm8, V, True, False, ["scm8", vk], [("ps", 6)])
                SNB = [(r1f(SN_OFF, 256), "sn"), (OFb[0], ("of", 0)), (OFb[1], ("of", 1))]
                for s in range(16):
                    b2 = s % 3
                    s0b = r1b(S0B_OFF + b2 * 128, 128) if b2 < 2 else r1b(11456, 128)
                    s0f = r1f(S0F_OFF + b2 * 256, 256) if b2 < 2 else r1f(11200, 256)
                    dma("pool", s0b, sgla[l, s, h], [], [("s0b", b2)], ("s0b", b2))
                    dma("pool", s0f, sgla[l, s, h], [], [("s0f", b2)], ("s0f", b2))
                    mm(PS[6][:, 0:256], Zv[:, s, :], s0b, False, s == 15, ["Z", ("s0b", b2)], [("ps", 6)])
                    pbank = 7 if s % 2 == 0 else 3
                    pss = PS[pbank][:, 0:256]
                    mm(pss, KMv[:, s, :], V, True, True, ["KM", vk], [("ps", pbank)])
                    sn, snk = SNB[s % 3]
                    tt("dve", sn, s0f, pss, ALU.add, [("s0f", b2), ("ps", pbank)], [snk])
                    ts("dve", sn, sn, ELs[:, s:s + 1], ALU.mult, [snk, "els"], [snk])
                    dma("sp", sgs[l, s, h], sn, [snk], [], ("sgs", s % 3))
                epilogue(PS[6][:, 0:256], [("ps", 6)], GGB9, ["ggb9"], h, 8, 0)

                sm = ggt
                dma("sp", sm, cc_s_out[l][h][0:128, :], [("ccsout", l, h)], ["ggt"], "sm")
                ts("dve", sm, sm, FLAG[:, 0:1], ALU.mult, ["ggt", "FLAG"], ["ggt"])
                cp("act", smb, sm, ["ggt"], ["smb"])
                act(STAT[:, 15:16], STAT[:, 14:15], AF.Exp, ["bacc"], ["ebt"], scale=-1.0 / 16.0)
                sf = r1f(SN_OFF, 256)
                stt("dve", sf, sm, STAT[:, 15:16], SST[:, hb, :], ALU.mult, ALU.add, ["ggt", "ebt", ("S", hb)], ["sn"])
                dma("sp", sgp[l, h], sf, ["sn"], [], ("sgs", 0))
                pending = h
                if h == 2:
                    pass

            corr_front(H - 1, 0)
            for t in range(8):
                if t + 1 < 8:
                    corr_front(H - 1, t + 1)
                corr_back(H - 1, t)

            def wout_load(krow0, cb, buf):
                wv, wkeys, slot = buf
                dma("pool", wv, w_out[l, krow0:krow0 + 1024, cb * 512:(cb + 1) * 512].rearrange("(k p) n -> p k n", p=128),
                    [], wkeys, slot, nobar=True)

            def wout_pass(krow0, bufs, preloaded=0, after_cb0=None):
                for cb in range(4):
                    if cb == 1 and after_cb0 is not None:
                        after_cb0()
                    buf = bufs[cb] if len(bufs) == 4 else bufs[cb % len(bufs)]
                    wv, wkeys, slot = buf
                    if cb >= preloaded:
                        wout_load(krow0, cb, buf)
                    for t in range(NT):
                        pb = PS[t % 2]
                        for k in range(8):
                            mm(pb[:], MTv[:, k, t * 128:(t + 1) * 128], wv[:, k, :], k == 0, k == 7,
                               [("mT", k, t), wkeys[0]], [("ps", t % 2)])
                        tt("dve", Xv[:, t, cb * 512:(cb + 1) * 512], Xv[:, t, cb * 512:(cb + 1) * 512], pb[:], ALU.add,
                           [("x", t, cb), ("ps", t % 2)], [("x", t, cb)])

            def u_loads(cbs=(0, 1)):
                for cb in cbs:
                    dma("pool", wb_view(cb, 512), w_in[l, :, C_U + cb * 512:C_U + (cb + 1) * 512].rearrange("(k p) n -> p k n", p=128),
                        [], [("wb", cb), ("wb", 0, 0), ("wb", 0, 1)] if cb == 0 else [("wb", cb)], ("wb", cb), nobar=True)

            chk(3)
            S.barrier()
            WO_R1 = [(r1b(4096 + i * 2048, 2048).rearrange("p (k n) -> p k n", n=512), [("wo", i)], ("wo", i)) for i in range(2)]
            WO_H0 = (WB[0][:, 0:4096].rearrange("p (k n) -> p k n", n=512), [("wb", 0, 0), ("wb", 0), ("wb4", 0)], ("wb", 0, 0))
            wout_load(0, 1, WO_R1[0])
            wout_load(0, 2, WO_R1[1])
            u_loads((1,))
            U78_OFF = 0
            SPB_OFF = 2048
            E_OFF, TA_OFF, TB_OFF = 4096, 5504, 6912
            DB_OFF = 8320
            WP_OFF = 9472
            PRE_OFF = 10496
            SPB = R1[0:120, SPB_OFF:SPB_OFF + 2048].rearrange("p (a c) -> p a c", c=1024)
            dma("sp", SPB, spool[l].rearrange("s t c -> (s t) c").rearrange("(a p) c -> p a c", p=120), [], ["SPB"], "spb")
            PRE = R1[0:15, PRE_OFF:PRE_OFF + 1024]
            WPv = r1b(WP_OFF, 1024).rearrange("p (g c d) -> p g c d", g=4, c=2)
            for g in range(4):
                dma("pool", WPv[:, g, :, :], w_pool[l, g].rearrange("(c p) d -> p c d", p=128), [], ["WP"], "wp")
            Dv = r1b(DB_OFF, 1152).rearrange("p (c t) -> p c t", t=T)
            wout_pass(0, [WO_H0, WO_R1[0], WO_R1[1], WO_R1[0]], preloaded=3, after_cb0=lambda: u_loads((0,)))
            S.barrier()
            chk(4)
            U78_OFF = 0
            SPB_OFF = 2048
            for ti, t in enumerate((7, 8)):
                u78 = r1f(U78_OFF + ti * 1024, 1024)
                for cb in range(2):
                    wv = wb_view(cb, 512)
                    for k in range(KC):
                        mm(PS[cb][:], HTv[:, k, t * 128:(t + 1) * 128], wv[:, k, :], k == 0, k == KC - 1,
                           [("hT", t), ("wb", cb)], [("ps", cb)])
                    cp("act" if cb == 0 else "dve", u78[:, cb * 512:(cb + 1) * 512], PS[cb][:], [("ps", cb)], [("u78", ti, cb)])
            u7 = r1f(U78_OFF, 1024)
            u8 = r1f(U78_OFF + 1024, 1024)
            u7k = [("u78", 0, 0), ("u78", 0, 1)]
            u8k = [("u78", 1, 0), ("u78", 1, 1)]
            dma("sp", spp[l], u7[113:128, :], u7k, [], "spp")
            dma("sp", cc_u_in[l][0:16, :], u7[112:128, :], u7k, [("ccuin", l)], "ccuin")
            A("pool", lambda e, l=l: e.collective_compute("AllGather", ALU.bypass, replica_groups=GROUPS,
                                                           ins=[cc_u_in[l]], outs=[cc_u_out[l]]),
              reads=[("ccuin", l)], writes=[("ccuout", l)], dma=True, slot=("ccu", l), inc=1)
            dma("sp", sps[l, :, 0:7, :], spool[l, :, 8:15, :], [], [], "sps0")
            for tt_ in range(8):
                srcap = R1[tt_:128:8, U78_OFF + 1024:U78_OFF + 2048]
                dma("sp", sps[l, :, 7 + tt_, :], srcap, u8k, [], "sps1")

            dma("sp", PRE, cc_u_out[l][1:16, :], [("ccuout", l)], ["PRE"], "pre")
            EOFFS = [4096, 0]

            def pool_proj(g, cc):
                ch = 2 * g + cc
                cb, j = ch // 4, ch % 4
                wv = wb_view(cb, 512)
                E = r1f(EOFFS[cc], 1407)
                ek_ = ("E", cc)
                ekw = [ek_] + ([("u78", 0, 0), ("u78", 0, 1), ("u78", 1, 0), ("u78", 1, 1)] if cc == 1 else [])
                Ep = E[:, 0:1039]
                Es = E[:, 1039:1407].rearrange("p (s c) -> p s c", c=23)
                for tb, (c0, cn) in enumerate(TB):
                    pb = PS[tb % 2]
                    for k in range(KC):
                        mm(pb[:, 0:cn], wv[:, k, j * 128:(j + 1) * 128], HTv[:, k, c0:c0 + cn], k == 0, k == KC - 1,
                           htk(tb_tiles(tb)) + [("wb", cb)], [("ps", tb % 2)])
                    if tb < 2:
                        cp("act", Ep[:, 15 + c0:15 + c0 + cn], pb[:, 0:cn], [("ps", tb % 2)], ekw)
                    else:
                        cp("act", Ep[:, 15 + c0:15 + 1024], pb[:, 0:256], [("ps", tb % 2)], [ek_])
                        cp("act", Es[:, :, 15:23], pb[:, 256:384].rearrange("p (s c) -> p s c", c=8), [("ps", tb % 2)], [ek_])

            def pool_windows(g, cc):
                w = 2 << g
                E = r1f(EOFFS[cc], 1407)
                ek_ = ("E", cc)
                Ep = E[:, 0:1039]
                Es = E[:, 1039:1407].rearrange("p (s c) -> p s c", c=23)
                ch = 2 * g + cc
                mm(PS[2][:, 0:15], PRE[:, ch * 128:(ch + 1) * 128], IDF[0:15, 0:15], True, True, ["PRE", "IDF"], [("ps", 2)])
                ts("dve", Ep[:, 0:15], PS[2][:, 0:15], FLAG[:, 0:1], ALU.mult, [("ps", 2), "FLAG"], [ek_])
                for a_ in range(2):
                    mm(PS[3][:, a_ * 120:(a_ + 1) * 120], SPB[:, a_, ch * 128:(ch + 1) * 128], IDF[0:120, 0:120], True, True,
                       ["SPB", "IDF"], [("ps", 3)])
                cp("dve", Es[:, :, 0:15], PS[3][:, 0:240].rearrange("p (s c) -> p s c", c=15), [("ps", 3)], [ek_])
                cur_p, cur_s, curk = Ep, Es, ek_
                bufs = [(TA_OFF, "TA"), (TB_OFF, "TB")]
                sh = 1
                step = 0
                while sh < w:
                    off, nk_ = bufs[step % 2]
                    Nt = r1f(off, 1407)
                    Np = Nt[:, 0:1039]
                    Ns = Nt[:, 1039:1407].rearrange("p (s c) -> p s c", c=23)
                    tt("pool", Np[:, sh:1039], cur_p[:, sh:1039], cur_p[:, 0:1039 - sh], ALU.add, [curk], [nk_])
                    tt("pool", Ns[:, :, sh:23], cur_s[:, :, sh:23], cur_s[:, :, 0:23 - sh], ALU.add, [curk], [nk_])
                    cur_p, cur_s, curk = Np, Ns, nk_
                    sh *= 2
                    step += 1
                stt("dve", Dv[:, cc, 0:1024], cur_p[:, 15:1039], 1.0 / w, Ep[:, 15:1039], ALU.mult, ALU.subtract,
                    [curk, ek_], [("D", cc)])
                stt("dve", Dv[:, cc, 1024:1152].rearrange("p (s c) -> p s c", c=8), cur_s[:, :, 15:23], 1.0 / w,
                    Es[:, :, 15:23], ALU.mult, ALU.subtract, [curk, ek_], [("D", cc)])
                t15 = STAT[:, 32:48]
                tt("dve", t15[:, 0:15], cur_p[:, 15:30], INVC[:, g * 15:(g + 1) * 15], ALU.mult, [curk, "INVC"], ["t15"])
                tt("dve", Dv[:, cc, 0:15], t15[:, 0:15], Ep[:, 15:30], ALU.subtract, ["t15", ek_, ("D", cc)], [("D", cc)])

            def pool_mm(g):
                for dc in range(2):
                    for tb, (c0, cn) in enumerate(TB):
                        pb = PS[4 + tb % 2]
                        for cc in range(2):
                            mm(pb[:, 0:cn], WPv[:, g, cc, dc * 128:(dc + 1) * 128], Dv[:, cc, c0:c0 + cn], cc == 0, cc == 1,
                               ["WP", ("D", cc)], [("ps", 4 + tb % 2)])
                        ts("dve", MTv[:, 2 * g + dc, c0:c0 + cn], pb[:, 0:cn], PSC[:, 2 * g + dc:2 * g + dc + 1], ALU.mult,
                           [("ps", 4 + tb % 2), "PSC"], [("mT", 2 * g + dc, t) for t in tb_tiles(tb)])

            WO_WB = [(WB[0][:, i * 4096:(i + 1) * 4096].rearrange("p (k n) -> p k n", n=512),
                      [("wb", 0, i), ("wb", 0), ("wb4", i)], ("wb", 0, i)) for i in range(2)]
            for g in range(4):
                pool_proj(g, 0)
                pool_proj(g, 1)
                if g == 1:
                    wout_load(1024, 0, WO_WB[0])
                    wout_load(1024, 1, WO_WB[1])
                if g > 0:
                    pool_mm(g - 1)
                pool_windows(g, 0)
                pool_windows(g, 1)
            pool_mm(3)
            S.barrier()
            ffn_loads(l, 0, alias=True, part=0)
            wout_pass(1024, WO_WB, preloaded=2)
            chk(5)

            ffn_loads(l, 0, alias=True, part=1)
            emit_norm(norm_ffn[l:l + 1, :])
            SG_OFF = [8192, 8576]
            ACTv = MT[:].rearrange("p (b f t) -> p b f t", b=2, f=FG)

            def down(gi, tile_done=None):
                b = gi % 2
                wd = r1b(WD_OFF[b], 4096).rearrange("p (f n) -> p f n", n=D)
                for t in range(NT):
                    if tile_done is not None and t > 0:
                        tile_done(t - 1)
                    for cb in range(4):
                        pb = PS[6 + (t * 4 + cb) % 2]
                        for f in range(FG):
                            mm(pb[:], ACTv[:, b, f, t * 128:(t + 1) * 128], wd[:, f, cb * 512:(cb + 1) * 512], f == 0, f == FG - 1,
                               [("act", b, f), ("wd", b)], [("ps", 6 + (t * 4 + cb) % 2)])
                        tt("dve", Xv[:, t, cb * 512:(cb + 1) * 512], Xv[:, t, cb * 512:(cb + 1) * 512], pb[:], ALU.add,
                           [("x", t, cb), ("ps", 6 + (t * 4 + cb) % 2)], [("x", t, cb)])

            for gi in range(NG):
                b = gi % 2
                if gi > 0:
                    ffn_loads(l, gi)
                for f in range(FG):
                    ff = gi * FG + f
                    slot = ffn_slot(ff)
                    wv = gu_view(slot)
                    for tb, (c0, cn) in enumerate(TB):
                        pg, pu = PS[tb], PS[3 + tb]
                        for k in range(KC):
                            mm(pg[:, 0:cn], wv[:, k, 0:128], HTv[:, k, c0:c0 + cn], k == 0, k == KC - 1,
                               htk(tb_tiles(tb)) + [("wb4", slot)], [("ps", tb)])
                        for k in range(KC):
                            mm(pu[:, 0:cn], wv[:, k, 128:256], HTv[:, k, c0:c0 + cn], k == 0, k == KC - 1,
                               htk(tb_tiles(tb)) + [("wb4", slot)], [("ps", 3 + tb)])
                        sg = r1f(SG_OFF[tb % 2], 384)
                        act(sg, pg[:, 0:cn], AF.Silu, [("ps", tb)], [("sg", tb % 2)])
                        tt("dve", ACTv[:, b, f, c0:c0 + cn], sg, pu[:, 0:cn], ALU.mult, [("sg", tb % 2), ("ps", 3 + tb)], [("act", b, f)])
                if gi > 0:
                    down(gi - 1)
            if l + 1 < depth:
                layer_prefetch(l + 1)
                down(NG - 1)
            else:
                GBF = r1f(4096, 2048)
                dma("sp", GBF, norm_final[0, :].partition_broadcast(128), [], ["gbf", ("wd", 1)], "gbc")

                def fin(t):
                    p = t % 2
                    junk = r1b(10240, 1024)
                    act(junk, Xv[:, t, :], AF.Square, xk(t), ["junk", ("ss", p)], scale=1.0 / np.sqrt(D), accum=STAT[:, p:p + 1])
                    act(STAT[:, 2 + p:3 + p], STAT[:, p:p + 1], AF.Ln, [("ss", p)], [("lnv", p)], bias=EPS)
                    act(STAT[:, 4 + p:5 + p], STAT[:, 2 + p:3 + p], AF.Exp, [("lnv", p)], [("rs", p)], scale=-0.5)
                    yt = r1f(6144 + p * D, D)
                    act(yt, Xv[:, t, :], AF.Copy, xk(t) + [("rs", p)], [("yt", p), ("wd", 1), ("sg", 0), ("sg", 1)],
                        scale=STAT[:, 4 + p:5 + p])
                    tt("pool", yt, yt, GBF, ALU.mult, [("yt", p), "gbf"], [("yt", p)])
                    dma("sp", y[t * 128:(t + 1) * 128, :], yt, [("yt", p)], [], ("yout", p))

                down(NG - 1, tile_done=fin)
                fin(NT - 1)
                final_done = True
            chk(6 + l)
        except _Stop:
            pass

        if not final_done:
            emit_norm(norm_final, to_y=True)
        info = S.emit()
    return nc, info


_CACHE = {}


def _consts(core):
    hf = core % 2
    tri = np.zeros((128, 256), np.float32)
    j = np.arange(128)[:, None]
    i = np.arange(128)[None, :]
    tri[:, 0:128] = (j <= i)
    tri[:, 128:256] = (j <= i) & ((j // 8) == (i // 8))
    seqsel = np.zeros((128, 32), np.float32)
    seqsel[:, 0] = 1.0
    for s in range(16):
        seqsel[8 * s:8 * s + 8, 16 + s] = 1.0
    rowmask = np.zeros((128, 16), np.float32)
    for s in range(16):
        rowmask[8 * s:8 * s + 8, s] = 1.0
    flag = np.full((128, 1), float(hf), np.float32)
    invc = np.zeros((128, 60), np.float32)
    for g, w in enumerate((2, 4, 8, 16)):
        for t in range(15):
            cnt = w if hf == 1 else min(w, t + 1)
            invc[:, g * 15 + t] = 1.0 / cnt
    return dict(c_identb=np.eye(128, dtype=np.float32).astype(ml_dtypes.bfloat16), c_identf=np.eye(128, dtype=np.float32),
                c_tri=tri, c_seqsel=seqsel, c_rowmask=rowmask, c_flag=flag, c_invc=invc)


def kernel(x_prompt, x_sample, state_gla, state_pool, norm_mix, w_in, w_gk_up, b_gk, gla_norm, w_pool,
           pool_scale, w_out, norm_ffn, w_gate, w_up, w_down, norm_final):
    f = lambda a: np.ascontiguousarray(np.asarray(a, dtype=np.float32))
    x_prompt, x_sample, state_gla, state_pool = f(x_prompt), f(x_sample), f(state_gla), f(state_pool)
    shared = dict(norm_mix=f(norm_mix), w_in=f(w_in), w_gk_up=f(w_gk_up), b_gk=f(b_gk), gla_norm=f(gla_norm),
                  w_pool=f(w_pool), pool_scale=f(pool_scale), w_out=f(w_out), norm_ffn=f(norm_ffn), w_gate=f(w_gate),
                  w_up=f(w_up), w_down=f(w_down), norm_final=f(norm_final).reshape(1, D))
    if "nc" not in _CACHE:
        import os
        _st = int(os.environ.get("MK_STAGE", "99"))
        _CACHE["nc"] = build_program(stage=_st)
        _CACHE["stage"] = _st
    nc, info = _CACHE["nc"]
    if _CACHE.get("stage", 99) < 6:
        for k_ in ("w_gate", "w_up", "w_down"):
            shared.pop(k_)
    in_maps = []
    for c in range(NCORES):
        b, hf = c // 2, c % 2
        xin = np.concatenate([x_prompt[b, hf * 1024:(hf + 1) * 1024], x_sample[16 * c:16 * c + 16].reshape(128, D)], 0)
        m = dict(shared)
        m["xin"] = np.ascontiguousarray(xin)
        m["sgla"] = np.ascontiguousarray(state_gla[:, 16 * c:16 * c + 16])
        m["spool"] = np.ascontiguousarray(state_pool[:, 16 * c:16 * c + 16])
        m.update(_consts(c))
        in_maps.append(m)
    res = run_bass_kernel_spmd(nc, in_maps, core_ids=list(range(NCORES)))
    y_prompt = np.zeros((4, 2048, D), np.float32)
    y_sample = np.zeros((128, 8, D), np.float32)
    sg_p = np.zeros((2, 4, H, DK, DV), np.float32)
    sp_p = np.zeros((2, 4, 15, 1024), np.float32)
    sg_s = np.zeros((2, 128, H, DK, DV), np.float32)
    sp_s = np.zeros((2, 128, 15, 1024), np.float32)
    for c in range(NCORES):
        r = res.results[c]
        b, hf = c // 2, c % 2
        y_prompt[b, hf * 1024:(hf + 1) * 1024] = r["y"][0:1024]
        y_sample[16 * c:16 * c + 16] = r["y"][1024:].reshape(16, 8, D)
        if hf == 1:
            sg_p[:, b] = r["sgp"]
            sp_p[:, b] = r["spp"]
        sg_s[:, 16 * c:16 * c + 16] = r["sgs"]
        sp_s[:, 16 * c:16 * c + 16] = r["sps"]
    return (y_prompt, y_sample, sg_p, sp_p, sg_s, sp_s)
```

```python
import contextlib
import numpy as np
import ml_dtypes
import concourse.bass as bass
import concourse.mybir as mybir
from concourse.bass_utils import run_bass_kernel_spmd

F32 = mybir.dt.float32
BF16 = mybir.dt.bfloat16
AF = mybir.ActivationFunctionType
ALU = mybir.AluOpType

NCORES = 8
T = 1152
NT = 9
D = 2048
KC = 16
DFF = 5632
NFF = 44
FG = 4
NG = NFF // FG
H = 4
DK = 128
DV = 256
EPS = 1e-6
C_Q, C_K, C_V, C_G, C_GK, C_U = 0, 512, 1024, 2048, 3072, 3088
INW = 4112


class _Op:
    __slots__ = ("eng", "fn", "deps", "dma", "slot", "sig", "sigval", "idx", "inc")


class Sched:
    def __init__(self, nc):
        self.nc = nc
        self.ops = []
        self.lw = {}
        self.rd = {}
        self.bar = {}
        self.last_on = {}
        self.dma_since_bar = []

    def barrier(self):
        deps = set(self.last_on.values()) | set(self.dma_since_bar)
        for e in ("pe", "act", "dve", "pool", "sp"):
            self.bar[e] = set(deps) | self.bar.get(e, set())
        self.dma_since_bar = []

    def add(self, eng, fn, reads=(), writes=(), dma=False, slot=None, inc=16, nobar=False):
        op = _Op()
        op.eng, op.fn, op.dma, op.slot = eng, fn, dma, slot
        op.inc = inc
        op.idx = len(self.ops)
        op.sig = False
        op.sigval = 0
        deps = set()
        ops = self.ops
        for r in reads:
            w = self.lw.get(r)
            if w is not None:
                deps.add(w)
        for wkey in writes:
            w = self.lw.get(wkey)
            if w is not None:
                o = ops[w]
                if o.dma and dma:
                    pass
                elif o.dma or dma or o.eng != eng or eng != "pe":
                    deps.add(w)
            for x in self.rd.get(wkey, ()):
                o = ops[x]
                if o.dma or dma or o.eng != eng or eng != "pe":
                    deps.add(x)
        b = self.bar.pop(eng, None)
        if b:
            deps |= b
        deps.discard(op.idx)
        red = {}
        for d_ in deps:
            o = ops[d_]
            key = ("s", o.slot) if o.dma else ("e", o.eng)
            if key not in red or red[key] < d_:
                red[key] = d_
        op.deps = set(red.values())
        ops.append(op)
        for r in reads:
            self.rd.setdefault(r, []).append(op.idx)
        for wkey in writes:
            self.lw[wkey] = op.idx
            self.rd[wkey] = []
        if dma:
            if not nobar:
                self.dma_since_bar.append(op.idx)
        else:
            self.last_on[eng] = op.idx
        return op.idx

    def emit(self):
        nc = self.nc
        ops = self.ops
        for op in ops:
            for d in op.deps:
                ops[d].sig = True
        engs = ["pe", "act", "dve", "pool", "sp"]
        cnt = {e: 0 for e in engs}
        slotcnt = {}
        for op in ops:
            if op.dma:
                slotcnt[op.slot] = slotcnt.get(op.slot, 0) + op.inc
                op.sigval = slotcnt[op.slot]
            elif op.sig:
                cnt[op.eng] += 1
                op.sigval = cnt[op.eng]
        slots = sorted(slotcnt.keys(), key=str)
        with contextlib.ExitStack() as es:
            esem = {e: es.enter_context(nc.semaphore("se_" + e)) for e in engs}
            ssem = {s: es.enter_context(nc.semaphore("sd_%d" % i)) for i, s in enumerate(slots)}
            block = es.enter_context(nc.Block())
            per_eng = {e: [op for op in ops if op.eng == e] for e in engs}

            def run(e, engobj):
                waited = {}
                for op in per_eng[e]:
                    need = {}
                    for d in op.deps:
                        o = ops[d]
                        if o.dma:
                            key = ("s", o.slot)
                            sem = ssem[o.slot]
                        else:
                            if o.eng == e and not op.dma and False:
                                continue
                            key = ("e", o.eng)
                            sem = esem[o.eng]
                        v = o.sigval
                        if waited.get(key, 0) >= v:
                            continue
                        if key not in need or need[key][1] < v:
                            need[key] = (sem, v)
                    for key, (sem, v) in need.items():
                        engobj.wait_ge(sem, v)
                        waited[key] = v
                    ins = op.fn(engobj)
                    if op.dma:
                        ins.then_inc(ssem[op.slot], op.inc)
                    elif op.sig:
                        ins.then_inc(esem[op.eng], 1)
                if e == "sp":
                    for s in slots:
                        if waited.get(("s", s), 0) < slotcnt[s]:
                            engobj.wait_ge(ssem[s], slotcnt[s])

            @block.tensor
            def _(eng):
                run("pe", eng)

            @block.scalar
            def _(eng):
                run("act", eng)

            @block.vector
            def _(eng):
                run("dve", eng)

            @block.gpsimd
            def _(eng):
                run("pool", eng)

            @block.sync
            def _(eng):
                run("sp", eng)
        return dict(n_ops=len(ops), n_sig=dict(cnt), n_slots=len(slots))


class _Stop(Exception):
    pass


def build_program(depth=2, stage=99):
    nc = bass.Bass("TRN2", target_bir_lowering=False)
    di = lambda n, s, dt=F32: nc.dram_tensor(n, s, dt, kind="ExternalInput").ap()
    do = lambda n, s, dt=F32: nc.dram_tensor(n, s, dt, kind="ExternalOutput").ap()
    xin = di("xin", [T, D])
    sgla = di("sgla", [2, 16, H, DK, DV])
    spool = di("spool", [2, 16, 15, 1024])
    norm_mix = di("norm_mix", [2, D])
    w_in = di("w_in", [2, D, INW])
    w_gk_up = di("w_gk_up", [2, 16, 512])
    b_gk = di("b_gk", [2, 512])
    gla_norm = di("gla_norm", [2, DV])
    w_pool = di("w_pool", [2, 4, 256, 256])
    pool_scale = di("pool_scale", [2, 1024])
    w_out = di("w_out", [2, D, D])
    norm_ffn = di("norm_ffn", [2, D])
    if stage >= 6:
        w_gate = di("w_gate", [2, D, DFF])
        w_up = di("w_up", [2, D, DFF])
        w_down = di("w_down", [2, DFF, D])
    norm_final = di("norm_final", [1, D])
    c_identb = di("c_identb", [128, 128], BF16)
    c_identf = di("c_identf", [128, 128])
    c_tri = di("c_tri", [128, 256])
    c_seqsel = di("c_seqsel", [128, 32])
    c_rowmask = di("c_rowmask", [128, 16])
    c_flag = di("c_flag", [128, 1])
    c_invc = di("c_invc", [128, 60])
    y = do("y", [T, D])
    sgp = do("sgp", [2, H, DK, DV])
    spp = do("spp", [2, 15, 1024])
    sgs = do("sgs", [2, 16, H, DK, DV])
    sps = do("sps", [2, 16, 15, 1024])
    cc_u_in = [nc.dram_tensor("cc_u_in%d" % l, [16, 1024], F32).ap() for l in range(2)]
    cc_u_out = [nc.dram_tensor("cc_u_out%d" % l, [32, 1024], F32).ap() for l in range(2)]
    cc_s_in = [[nc.dram_tensor("cc_s_in%d_%d" % (l, h), [128, 256], F32).ap() for h in range(H)] for l in range(2)]
    cc_s_out = [[nc.dram_tensor("cc_s_out%d_%d" % (l, h), [256, 256], F32).ap() for h in range(H)] for l in range(2)]
    GROUPS = [[2 * i, 2 * i + 1] for i in range(NCORES // 2)]

    es = contextlib.ExitStack()
    with es:
        sb = lambda name, shape, dt: es.enter_context(nc.sbuf_tensor(name, shape, dt))
        X = sb("X", [128, NT * D], F32)
        HT = sb("HT", [128, KC * T], BF16)
        MT = sb("MT", [128, 8 * T], BF16)
        WB = [sb("WB0", [128, KC * 512], BF16), sb("WB1", [128, KC * 512], BF16)]
        IDB = sb("IDB", [128, 128], BF16)
        IDF = sb("IDF", [128, 128], F32)
        TRI = sb("TRI", [128, 256], F32)
        SEQSEL = sb("SEQSEL", [128, 32], F32)
        ROWMASK = sb("ROWMASK", [128, 16], F32)
        FLAG = sb("FLAG", [128, 1], F32)
        INVC = sb("INVC", [128, 60], F32)
        PSC = sb("PSC", [128, 8], F32)
        GNB = sb("GNB", [128, 256], F32)
        WGKLR = sb("WGKLR", [128, KC * 16], BF16)
        STAT = sb("STAT", [128, 48], F32)
        R1W = 11736
        R1 = sb("R1", [128, R1W], F32)
        PS = [es.enter_context(nc.psum_tensor("PS%d" % i, [128, 512], F32)) for i in range(8)]

        def r1f(off, n, parts=128):
            return R1[0:parts, off:off + n]

        def r1b(off, n, parts=128):
            return R1[0:parts, off:off + n].bitcast(BF16)

        Xv = X[:].rearrange("p (t d) -> p t d", d=D)
        HTv = HT[:].rearrange("p (k t) -> p k t", t=T)
        MTv = MT[:].rearrange("p (k t) -> p k t", t=T)
        GBC = MT[:, 0:2 * D].bitcast(F32)

        S = Sched(nc)
        A = S.add

        def dma(q, out, in_, reads, writes, slot, slow=False, nobar=False):
            if slow:
                A(q, lambda e: e.dma_start(out=out, in_=in_, allow_slow_non_contiguous=True), reads=reads, writes=writes, dma=True, slot=slot, nobar=nobar)
            else:
                A(q, lambda e: e.dma_start(out=out, in_=in_), reads=reads, writes=writes, dma=True, slot=slot, nobar=nobar)

        def mm(out, lhsT, rhs, start, stop, reads, writes):
            A("pe", lambda e: e.matmul(out, lhsT=lhsT, rhs=rhs, start=start, stop=stop), reads=reads, writes=writes)

        def tr(out, in_, reads, writes):
            A("pe", lambda e: e.transpose(out, in_, IDB[:]), reads=list(reads) + ["IDB"], writes=writes)

        def act(out, in_, func, reads, writes, scale=1.0, bias=0.0, accum=None):
            if accum is None:
                A("act", lambda e: e.activation(out=out, in_=in_, func=func, bias=bias, scale=scale), reads=reads, writes=writes)
            else:
                A("act", lambda e: e.activation(out=out, in_=in_, func=func, bias=bias, scale=scale, accum_out=accum), reads=reads, writes=writes)

        def tt(eng, out, in0, in1, op, reads, writes):
            A(eng, lambda e: e.tensor_tensor(out=out, in0=in0, in1=in1, op=op), reads=reads, writes=writes)

        def ts(eng, out, in0, s1, op0, reads, writes):
            if op0 == ALU.mult:
                A(eng, lambda e: e.tensor_scalar_mul(out, in0, s1), reads=reads, writes=writes)
            else:
                A(eng, lambda e: e.tensor_scalar_add(out, in0, s1), reads=reads, writes=writes)

        def stt(eng, out, in0, scalar, in1, op0, op1, reads, writes):
            A(eng, lambda e: e.scalar_tensor_tensor(out=out, in0=in0, scalar=scalar, in1=in1, op0=op0, op1=op1), reads=reads, writes=writes)

        def cp(eng, out, in_, reads, writes):
            if eng == "act":
                A("act", lambda e: e.copy(out=out, in_=in_), reads=reads, writes=writes)
            else:
                A(eng, lambda e: e.tensor_copy(out=out, in_=in_), reads=reads, writes=writes)

        def memset(eng, ap, val, writes):
            A(eng, lambda e: e.memset(ap, val), writes=writes)

        xk = lambda t: [("x", t, cb) for cb in range(4)]
        htk = lambda tiles: [("hT", t) for t in tiles]
        TB = [(0, 384), (384, 384), (768, 384)]
        tb_tiles = lambda tb: [3 * tb, 3 * tb + 1, 3 * tb + 2]

        dma("sp", IDB[:], c_identb, [], ["IDB"], "c0")
        dma("sp", IDF[:], c_identf, [], ["IDF"], "c1")
        dma("sp", TRI[:], c_tri, [], ["TRI"], "c2")
        dma("sp", SEQSEL[:], c_seqsel, [], ["SEQSEL"], "c3")
        dma("sp", ROWMASK[:], c_rowmask, [], ["ROWMASK"], "c4")
        dma("sp", FLAG[:], c_flag, [], ["FLAG"], "c5")
        dma("sp", INVC[:], c_invc, [], ["INVC"], "c6")
        dma("sp", GBC, norm_mix[0, :].partition_broadcast(128), [], ["gbc"], "gbc")
        for t in range(NT):
            dma("sp", Xv[:, t, :], xin[t * 128:(t + 1) * 128, :], [], xk(t), ("xin", t), nobar=True)

        XN_OFF, JUNK_OFF, YT_OFF = 0, 1024, 2048

        def emit_norm(gvec, to_y=False, skip_gbc=False):
            S.barrier()
            if not skip_gbc:
                dma("sp", GBC, gvec[0, :].partition_broadcast(128), [], ["gbc"], "gbc")
            XNb = [r1b(8192, 1024), r1b(9216, 1024)]
            junk = r1b(10240, 1024)

            def front(t):
                p = t % 2
                act(junk, Xv[:, t, :], AF.Square, xk(t), ["junk", ("ss", p)], scale=1.0 / np.sqrt(D), accum=STAT[:, p:p + 1])
                act(STAT[:, 2 + p:3 + p], STAT[:, p:p + 1], AF.Ln, [("ss", p)], [("lnv", p)], bias=EPS)
                act(STAT[:, 4 + p:5 + p], STAT[:, 2 + p:3 + p], AF.Exp, [("lnv", p)], [("rs", p)], scale=-0.5)
                rs = STAT[:, 4 + p:5 + p]
                if to_y:
                    yt = r1f(YT_OFF + p * D, D)
                    stt("dve", yt, Xv[:, t, :], rs, GBC, ALU.mult, ALU.mult, xk(t) + [("rs", p), "gbc"], [("yt", p)])
                    dma("sp", y[t * 128:(t + 1) * 128, :], yt, [("yt", p)], [], ("yout", p))
                else:
                    stt("dve", XNb[p], Xv[:, t, :], rs, GBC, ALU.mult, ALU.mult, xk(t) + [("rs", p), "gbc"], [("xn", p)])

            def back(t):
                p = t % 2
                xn = XNb[p]
                for half in range(2):
                    bank = 4 + 2 * p + half
                    pb = PS[bank][:].bitcast(BF16)
                    for j in range(8):
                        k = half * 8 + j
                        tr(pb[:, j * 128:(j + 1) * 128], xn[:, k * 128:(k + 1) * 128], [("xn", p)], [("ps", bank)])
                    dst = HTv[:, half * 8:(half + 1) * 8, t * 128:(t + 1) * 128]
                    src = pb.rearrange("p (k c) -> p k c", c=128)
                    cp("act" if half == 0 else "dve", dst, src, [("ps", bank)], [("hT", t)])

            front(0)
            for t in range(NT):
                if t + 1 < NT:
                    front(t + 1)
                if not to_y:
                    back(t)
            S.barrier()

        def load_w_cols(slot, dst_col0, src2d, ncols, nk=KC):
            pass

        def wb_view(slot, width, nk=KC):
            return WB[slot][:, 0:nk * width].rearrange("p (k n) -> p k n", n=width)

        def load_head_w(l, h, alias=False):
            hb = h % 2
            wa = WB[0][:, hb * 4096:(hb + 1) * 4096].rearrange("p (k n) -> p k n", n=256)
            wbv = wb_view(1, 512)
            wak = ("wb", 0, hb)
            ka = [wak] + ([("wb", 0), ("wb4", 0), ("wb4", 1)] if alias else [])
            kb = [("wb", 1)] + ([("wb4", 2), ("wb4", 3)] if alias else [])
            srcs = [(wa, 0, C_Q + h * 128, 128, ka, wak), (wa, 128, C_K + h * 128, 128, ka, wak),
                    (wbv, 0, C_V + h * 256, 256, kb, ("wb", 1)), (wbv, 256, C_G + h * 256, 256, kb, ("wb", 1))]
            for (dst, d0, c0, cn, wkeys, slot) in srcs:
                dma("pool", dst[:, :, d0:d0 + cn], w_in[l, :, c0:c0 + cn].rearrange("(k p) n -> p k n", p=128),
                    [], wkeys, slot, nobar=True)

        def layer_prefetch(l):
            dma("sp", PSC[:], pool_scale[l, :].rearrange("(k p) -> p k", p=128), [], ["PSC"], "psc", slow=True)
            dma("sp", GNB[:], gla_norm[l, :].partition_broadcast(128), [], ["GNB"], "gnb")
            dma("pool", WGKLR[:].rearrange("p (k n) -> p k n", n=16),
                w_in[l, :, C_GK:C_GK + 16].rearrange("(k p) n -> p k n", p=128), [], ["WGKLR"], "wgklr")
            load_head_w(l, 0, alias=True)

        WD_OFF = [0, 4096]

        def gu_view(slot):
            return WB[slot // 2][:, (slot % 2) * 4096:(slot % 2 + 1) * 4096].rearrange("p (k n) -> p k n", n=256)

        def ffn_slot(ff):
            return ff % 4

        def ffn_loads(l, gi, alias=False, part=None):
            b = gi % 2
            for f in range(FG):
                if part == 0 and f >= 2:
                    continue
                if part == 1 and f < 2:
                    continue
                ff = gi * FG + f
                slot = ffn_slot(ff)
                wv = gu_view(slot)
                wk = [("wb4", slot)] + (([("wb", 0), ("wb", 0, slot)] if slot < 2 else [("wb", 1), ("wb1h", slot - 2)]) if alias else [])
                dma("pool", wv[:, :, 0:128], w_gate[l, :, ff * 128:(ff + 1) * 128].rearrange("(k p) n -> p k n", p=128),
                    [], wk, ("wb4", slot), nobar=True)
                dma("pool", wv[:, :, 128:256], w_up[l, :, ff * 128:(ff + 1) * 128].rearrange("(k p) n -> p k n", p=128),
                    [], wk, ("wb4", slot), nobar=True)
            if part == 0:
                return
            wd = r1b(WD_OFF[b], 4096).rearrange("p (f n) -> p f n", n=D)
            dma("pool", wd, w_down[l, gi * FG * 128:(gi + 1) * FG * 128, :].rearrange("(f p) n -> p f n", p=128),
                [], [("wd", b)], ("wd", b), nobar=True)

        def chk(n):
            if stage <= n:
                raise _Stop()

        layer_prefetch(0)
        final_done = False
        try:
          for l in range(depth):
            chk(0)
            emit_norm(norm_mix[l:l + 1, :], skip_gbc=(l == 0))
            chk(1)
            GKL_OFF = 0
            WGK_OFF = 576
            ORAW_OFF = 832
            GGB_OFF = 2880
            QS_OFF = 3904
            SST_OFF = 4416
            SBF_OFF = 4928
            S0F_OFF = 5184
            S0B_OFF = 5696
            Z_OFF = 5952
            KM_OFF = 7040
            V_OFF = 8064
            GGT_OFF = 8448
            SP_OFF = 8704
            EQ_OFF = 8832
            EK_OFF = 8960
            QKT_OFF = 9088
            QKTT_OFF = 9344
            SCM_OFF = 9600
            JK_OFF = 9664
            OG_OFF = 9792
            SN_OFF = 10048
            OF_OFF = 10304
            SM_OFF = 10816
            SMB_OFF = 11072
            assert SMB_OFF + 128 <= R1W

            GKL = r1b(GKL_OFF, 576, 17)
            WGK = r1b(WGK_OFF, 256, 17)
            WGKF = r1f(ORAW_OFF, 512, 17)
            memset("dve", GKL, 1.0, ["GKL"])
            dma("sp", WGKF[0:16, :], w_gk_up[l], [], ["WGKF"], "wgkf")
            dma("sp", WGKF[16:17, :], b_gk[l:l + 1, :], [], ["WGKF"], "wgkf")
            cp("dve", WGK, WGKF, ["WGKF"], ["WGK"])
            wlr = WGKLR[:].rearrange("p (k n) -> p k n", n=16)
            for tb, (c0, cn) in enumerate(TB):
                for k in range(KC):
                    mm(PS[2][0:16, 0:cn], wlr[:, k, :], HTv[:, k, c0:c0 + cn], k == 0, k == KC - 1,
                       htk(tb_tiles(tb)) + ["WGKLR"], [("ps", 2)])
                cp("act", GKL[0:16, c0:c0 + cn], PS[2][0:16, 0:cn], [("ps", 2)], ["GKL"])

            Zfull = r1b(Z_OFF, 1088)
            memset("pool", Zfull, 0.0, ["Z"])
            Zdiag = Zfull.rearrange("p (s c) -> p s c", c=136)[:, :, 0:8]
            Zv = Zfull[:, 0:2048].rearrange("p (s t) -> p s t", t=128)
            KMv = r1b(KM_OFF, 1024).rearrange("p (s d) -> p s d", d=128)
            S.barrier()
            SST = r1f(SST_OFF, 512).rearrange("p (h v) -> p h v", v=256)
            SBF = r1b(SBF_OFF, 256).rearrange("p (h v) -> p h v", v=256)
            ORAW = r1f(ORAW_OFF, 2048).rearrange("p (t v) -> p t v", v=256)
            GGB = r1b(GGB_OFF, 1024).rearrange("p (t v) -> p t v", v=256)
            QS = r1b(QS_OFF, 512)
            Vb = [r1b(V_OFF + i * 128, 128) for i in range(3)]
            QKT = [r1b(QKT_OFF + i * 128, 128) for i in range(2)]
            QKTT = [r1b(QKTT_OFF + i * 128, 128) for i in range(2)]
            OGb = [r1b(OG_OFF + i * 128, 128) for i in range(2)]
            OFb = [r1f(OF_OFF + i * 256, 256) for i in range(2)]
            ggt = r1f(GGT_OFF, 256)
            sp_ = r1f(SP_OFF, 128)
            eq = r1f(EQ_OFF, 128)
            ek = r1f(EK_OFF, 128)
            scm = r1b(SCM_OFF, 64)
            ELs = STAT[:, 16:32]
            V8 = r1b(JK_OFF, 128)
            GGB9 = r1b(SM_OFF, 128)
            scm8 = r1b(SM_OFF + 128, 64)

            def epi_front(osrc, okeys, gg, ggkeys, p):
                og = OGb[p]
                c = 8 + 3 * p
                act(og, osrc, AF.Square, okeys, [("og", p), ("e_ss", p)], scale=1.0 / 16.0, accum=STAT[:, c:c + 1])
                act(STAT[:, c + 1:c + 2], STAT[:, c:c + 1], AF.Ln, [("e_ss", p)], [("e_ln", p)], bias=EPS)
                act(STAT[:, c + 2:c + 3], STAT[:, c + 1:c + 2], AF.Exp, [("e_ln", p)], [("e_rs", p)], scale=-0.5)
                stt("dve", og, osrc, STAT[:, c + 2:c + 3], gg, ALU.mult, ALU.mult, okeys + [("e_rs", p)] + ggkeys, [("og", p)])

            def epi_back(h, t, p):
                og = OGb[p]
                pb = PS[5][:, 256 + 128 * p:384 + 128 * p].bitcast(BF16)
                for j in range(2):
                    tr(pb[:, j * 128:(j + 1) * 128], og[:, j * 128:(j + 1) * 128], [("og", p)], [("ps", 5)])
                cp("act", MTv[:, 2 * h:2 * h + 2, t * 128:(t + 1) * 128], pb.rearrange("p (k c) -> p k c", c=128),
                   [("ps", 5)], [("mT", 2 * h, t), ("mT", 2 * h + 1, t)])

            def epilogue(osrc, okeys, gg, ggkeys, h, t, p):
                epi_front(osrc, okeys, gg, ggkeys, p)
                epi_back(h, t, p)

            smb = r1b(SMB_OFF, 128)
            PC = [(6, PS[6][:, 256:512]), (7, PS[7][:, 256:512])]

            def corr_front(hp, t):
                bank, reg = PC[t % 2]
                mm(reg, QS[:, t * 128:(t + 1) * 128], smb, True, True, [("qs", t), "smb"], [("ps", bank)])
                of = OFb[t % 2]
                tt("dve", of, ORAW[:, t, :], reg, ALU.add, [("oraw", t), ("ps", bank)], [("of", t % 2)])
                epi_front(of, [("of", t % 2)], GGB[:, t, :], [("ggb", t)], t % 2)

            def corr_back(hp, t):
                epi_back(hp, t, t % 2)

            pending = None
            for h in range(H):
                hb = h % 2
                wa = WB[0][:, hb * 4096:(hb + 1) * 4096].rearrange("p (k n) -> p k n", n=256)
                wbv = wb_view(1, 512)
                wak = ("wb", 0, hb)
                if h == H - 1:
                    dma("pool", WB[0][:, 0:4096].rearrange("p (k n) -> p k n", n=512),
                        w_out[l, 0:1024, 0:512].rearrange("(k p) n -> p k n", p=128),
                        [], [("wb", 0, 0), ("wb", 0), ("wb4", 0)], ("wb", 0, 0), nobar=True)
                memset("dve", SST[:, hb, :], 0.0, [("S", hb)])
                memset("pool", SBF[:, hb, :], 0.0, [("Sb", hb)])
                memset("dve", STAT[:, 14:15], 0.0, ["bacc"])

                def P(t):
                    return 0 if t == 8 else t + 1

                def tinfo(t):
                    smp = (t == 8)
                    v_ = 1 if smp else 0
                    return smp, v_, (16 if smp else 1), TRI[:, v_ * 128:(v_ + 1) * 128]

                def S1(t):
                    smp, v_, nseq, tri = tinfo(t)
                    pq = PS[2 + P(t) % 2]
                    pvg = PS[P(t) % 2]
                    for k in range(KC):
                        mm(pq[:, 0:256], HTv[:, k, t * 128:(t + 1) * 128], wa[:, k, :], k == 0, k == KC - 1,
                           [("hT", t), wak], [("ps", 2 + P(t) % 2)])
                    mm(pq[:, 256:384], GKL[:, t * 128:(t + 1) * 128], WGK[:, h * 128:(h + 1) * 128], True, True,
                       ["GKL", "WGK"], [("ps", 2 + P(t) % 2)])
                    for k in range(KC):
                        mm(pvg[:], HTv[:, k, t * 128:(t + 1) * 128], wbv[:, k, :], k == 0, k == KC - 1,
                           [("hT", t), ("wb", 1)], [("ps", P(t) % 2)])
                    act(sp_, pq[:, 256:384], AF.Exp, [("ps", 2 + P(t) % 2)], ["sp"], scale=-1.0)
                    act(sp_, sp_, AF.Ln, ["sp"], ["sp"], bias=1.0)
                    V = V8 if smp else Vb[P(t) % 3]
                    cp("act", V, pvg[:, 0:256], [("ps", P(t) % 2)], ["V8" if smp else ("V", P(t) % 3)])
                    act(ggt, pvg[:, 256:512], AF.Exp, [("ps", P(t) % 2)], ["ggt"], scale=-1.0)
                    ts("dve", ggt, ggt, 1.0, ALU.add, ["ggt"], ["ggt"])
                    A("dve", lambda e: e.reciprocal(out=ggt, in_=ggt), reads=["ggt"], writes=["ggt"])
                    tt("dve", ggt, ggt, pvg[:, 256:512], ALU.mult, ["ggt", ("ps", P(t) % 2)], ["ggt"])
                    if smp:
                        tt("dve", GGB9, ggt, GNB[:], ALU.mult, ["ggt", "GNB"], ["ggb9"])
                    else:
                        tt("dve", GGB[:, t, :], ggt, GNB[:], ALU.mult, ["ggt", "GNB"], [("ggb", t)])

                def S2(t):
                    smp, v_, nseq, tri = tinfo(t)
                    pq = PS[2 + P(t) % 2]
                    mm(PS[4][:, 0:128], tri, sp_, True, True, ["TRI", "sp"], [("ps", 4)])
                    sel = SEQSEL[:, v_ * 16:v_ * 16 + nseq]
                    mm(PS[4][:, 128:128 + nseq], sp_, sel, True, True, ["sp", "SEQSEL"], [("ps", 4)])
                    act(eq, PS[4][:, 0:128], AF.Exp, [("ps", 4)], ["eq"], scale=-1.0 / 16.0)
                    act(ek, PS[4][:, 0:128], AF.Exp, [("ps", 4)], ["ek"], scale=1.0 / 16.0)
                    if smp:
                        act(ELs, PS[4][:, 128:144], AF.Exp, [("ps", 4)], ["els"], scale=-1.0 / 16.0)
                    else:
                        act(STAT[:, 6 + P(t) % 2:7 + P(t) % 2], PS[4][:, 128:129], AF.Exp, [("ps", 4)], [("elp", P(t) % 2)], scale=-1.0 / 16.0)
                        act(STAT[:, 32 + P(t) % 2:33 + P(t) % 2], STAT[:, 14:15], AF.Exp, ["bacc"], [("eb", P(t) % 2)], scale=-1.0 / 16.0)
                        tt("dve", STAT[:, 14:15], STAT[:, 14:15], PS[4][:, 128:129], ALU.add, ["bacc", ("ps", 4)], ["bacc"])
                    qkt = QKT[P(t) % 2]
                    stt("dve", qkt[:, 0:128], pq[:, 0:128], float(DK ** -0.5), eq, ALU.mult, ALU.mult,
                        [("ps", 2 + P(t) % 2), "eq"], [("qkt", P(t) % 2)])
                    tt("dve", qkt[:, 128:256], pq[:, 128:256], ek, ALU.mult, [("ps", 2 + P(t) % 2), "ek"], [("qkt", P(t) % 2)])

                def S3(t):
                    smp, v_, nseq, tri = tinfo(t)
                    qkt = QKT[P(t) % 2]
                    qktt = QKTT[P(t) % 2]
                    pbt = PS[5][:, 0:128].bitcast(BF16)
                    tr(pbt[:, 0:128], qkt[:, 0:128], [("qkt", P(t) % 2)], [("ps", 5)])
                    tr(pbt[:, 128:256], qkt[:, 128:256], [("qkt", P(t) % 2)], [("ps", 5)])
                    cp("act", qktt, pbt, [("ps", 5)], [("qktt", P(t) % 2)])
                    if not smp:
                        act(QS[:, t * 128:(t + 1) * 128], pbt[:, 0:128], AF.Copy, [("ps", 5), ("eb", P(t) % 2)], [("qs", t)],
                            scale=STAT[:, 32 + P(t) % 2:33 + P(t) % 2])

                def S4(t):
                    smp, v_, nseq, tri = tinfo(t)
                    qktt = QKTT[P(t) % 2]
                    mm(PS[5][:, 128:256], qktt[:, 128:256], qktt[:, 0:128], True, True, [("qktt", P(t) % 2)], [("ps", 5)])
                    if smp:
                        tt("dve", scm8, PS[5][:, 128:256], tri, ALU.mult, [("ps", 5), "TRI"], ["scm8"])
                    else:
                        tt("dve", scm, PS[5][:, 128:256], tri, ALU.mult, [("ps", 5), "TRI"], ["scm"])

                def S5(t):
                    qkt = QKT[P(t) % 2]
                    qktt = QKTT[P(t) % 2]
                    V = Vb[P(t) % 3]
                    vk = ("V", P(t) % 3)
                    mm(PS[6][:, 0:256], scm, V, True, False, ["scm", vk], [("ps", 6)])
                    mm(PS[6][:, 0:256], qktt[:, 0:128], SBF[:, hb, :], False, True, [("qktt", P(t) % 2), ("Sb", hb)], [("ps", 6)])
                    cp("act", ORAW[:, t, :], PS[6][:, 0:256], [("ps", 6)], [("oraw", t)])
                    mm(PS[7][:, 0:256], qkt[:, 128:256], V, True, True, [("qkt", P(t) % 2), vk], [("ps", 7)])
                    tt("dve", SST[:, hb, :], SST[:, hb, :], PS[7][:, 0:256], ALU.add, [("S", hb), ("ps", 7)], [("S", hb)])
                    ts("dve", SST[:, hb, :], SST[:, hb, :], STAT[:, 6 + P(t) % 2:7 + P(t) % 2], ALU.mult, [("S", hb), ("elp", P(t) % 2)], [("S", hb)])
                    cp("pool", SBF[:, hb, :], SST[:, hb, :], [("S", hb)], [("Sb", hb)])

                def S5z(t):
                    qkt = QKT[P(t) % 2]
                    qktt = QKTT[P(t) % 2]
                    cp("pool", Zdiag, qktt[:, 0:128].rearrange("p (s j) -> p s j", j=8), [("qktt", P(t) % 2)], ["Z"])
                    A("dve", lambda e, qkt=qkt: e.tensor_tensor(
                        out=KMv, in0=qkt[:, 128:256].unsqueeze(1).to_broadcast([128, 16, 128]),
                        in1=ROWMASK[:].unsqueeze(2).to_broadcast([128, 16, 128]), op=ALU.mult),
                      reads=[("qkt", P(t) % 2), "ROWMASK"], writes=["KM"])

                tl = [8] + list(range(8))
                NP_ = len(tl)
                for i in range(NP_ + 2):
                    if pending is not None and i < 8:
                        corr_front(pending, i)
                    if 0 <= i - 1 < NP_:
                        S2(tl[i - 1])
                    if i < NP_:
                        S1(tl[i])
                    if i == NP_ - 1 and h + 1 < H:
                        load_head_w(l, h + 1)
                    if pending is not None and i < 8:
                        corr_back(pending, i)
                    if 0 <= i - 1 < NP_:
                        S3(tl[i - 1])
                    if 0 <= i - 2 < NP_:
                        if tl[i - 2] == 8:
                            S5z(8)
                        else:
                            S5(tl[i - 2])
                    if 0 <= i - 1 < NP_:
                        S4(tl[i - 1])

                dma("sp", cc_s_in[l][h], SST[:, hb, :], [("S", hb)], [("ccsin", l, h)], "ccsin")
                A("pool", lambda e, l=l, h=h: e.collective_compute("AllGather", ALU.bypass, replica_groups=GROUPS,
                                                                    ins=[cc_s_in[l][h]], outs=[cc_s_out[l][h]]),
                  reads=[("ccsin", l, h)], writes=[("ccsout", l, h)], dma=True, slot=("ccs", l, h), inc=1)

                t = 8
                V = V8
                vk = "V8"
                mm(PS[6][:, 0:256], scm8, V, True, False, ["scm8", vk], [("ps", 6)])
                SNB = [(r1f(SN_OFF, 256), "sn"), (OFb[0], ("of", 0)), (OFb[1], ("of", 1))]
                for s in range(16):
                    b2 = s % 3
                    s0b = r1b(S0B_OFF + b2 * 128, 128) if b2 < 2 else r1b(11456, 128)
                    s0f = r1f(S0F_OFF + b2 * 256, 256) if b2 < 2 else r1f(11200, 256)
                    dma("pool", s0b, sgla[l, s, h], [], [("s0b", b2)], ("s0b", b2))
                    dma("pool", s0f, sgla[l, s, h], [], [("s0f", b2)], ("s0f", b2))
                    mm(PS[6][:, 0:256], Zv[:, s, :], s0b, False, s == 15, ["Z", ("s0b", b2)], [("ps", 6)])
                    pbank = 7 if s % 2 == 0 else 3
                    pss = PS[pbank][:, 0:256]
                    mm(pss, KMv[:, s, :], V, True, True, ["KM", vk], [("ps", pbank)])
                    sn, snk = SNB[s % 3]
                    tt("dve", sn, s0f, pss, ALU.add, [("s0f", b2), ("ps", pbank)], [snk])
                    ts("dve", sn, sn, ELs[:, s:s + 1], ALU.mult, [snk, "els"], [snk])
                    dma("sp", sgs[l, s, h], sn, [snk], [], ("sgs", s % 3))
                epilogue(PS[6][:, 0:256], [("ps", 6)], GGB9, ["ggb9"], h, 8, 0)

                sm = ggt
                dma("sp", sm, cc_s_out[l][h][0:128, :], [("ccsout", l, h)], ["ggt"], "sm")
                ts("dve", sm, sm, FLAG[:, 0:1], ALU.mult, ["ggt", "FLAG"], ["ggt"])
                cp("act", smb, sm, ["ggt"], ["smb"])
                act(STAT[:, 15:16], STAT[:, 14:15], AF.Exp, ["bacc"], ["ebt"], scale=-1.0 / 16.0)
                sf = r1f(SN_OFF, 256)
                stt("dve", sf, sm, STAT[:, 15:16], SST[:, hb, :], ALU.mult, ALU.add, ["ggt", "ebt", ("S", hb)], ["sn"])
                dma("sp", sgp[l, h], sf, ["sn"], [], ("sgs", 0))
                pending = h
                if h == 2:
                    pass

            corr_front(H - 1, 0)
            for t in range(8):
                if t + 1 < 8:
                    corr_front(H - 1, t + 1)
                corr_back(H - 1, t)

            def wout_load(krow0, cb, buf):
                wv, wkeys, slot = buf
                dma("pool", wv, w_out[l, krow0:krow0 + 1024, cb * 512:(cb + 1) * 512].rearrange("(k p) n -> p k n", p=128),
                    [], wkeys, slot, nobar=True)

            def wout_pass(krow0, bufs, preloaded=0, after_cb0=None):
                for cb in range(4):
                    if cb == 1 and after_cb0 is not None:
                        after_cb0()
                    buf = bufs[cb] if len(bufs) == 4 else bufs[cb % len(bufs)]
                    wv, wkeys, slot = buf
                    if cb >= preloaded:
                        wout_load(krow0, cb, buf)
                    for t in range(NT):
                        pb = PS[t % 2]
                        for k in range(8):
                            mm(pb[:], MTv[:, k, t * 128:(t + 1) * 128], wv[:, k, :], k == 0, k == 7,
                               [("mT", k, t), wkeys[0]], [("ps", t % 2)])
                        tt("dve", Xv[:, t, cb * 512:(cb + 1) * 512], Xv[:, t, cb * 512:(cb + 1) * 512], pb[:], ALU.add,
                           [("x", t, cb), ("ps", t % 2)], [("x", t, cb)])

            def u_loads(cbs=(0, 1)):
                for cb in cbs:
                    dma("pool", wb_view(cb, 512), w_in[l, :, C_U + cb * 512:C_U + (cb + 1) * 512].rearrange("(k p) n -> p k n", p=128),
                        [], [("wb", cb), ("wb", 0, 0), ("wb", 0, 1)] if cb == 0 else [("wb", cb)], ("wb", cb), nobar=True)

            chk(3)
            S.barrier()
            WO_R1 = [(r1b(4096 + i * 2048, 2048).rearrange("p (k n) -> p k n", n=512), [("wo", i)], ("wo", i)) for i in range(2)]
            WO_H0 = (WB[0][:, 0:4096].rearrange("p (k n) -> p k n", n=512), [("wb", 0, 0), ("wb", 0), ("wb4", 0)], ("wb", 0, 0))
            wout_load(0, 1, WO_R1[0])
            wout_load(0, 2, WO_R1[1])
            u_loads((1,))
            U78_OFF = 0
            SPB_OFF = 2048
            E_OFF, TA_OFF, TB_OFF = 4096, 5504, 6912
            DB_OFF = 8320
            WP_OFF = 9472
            PRE_OFF = 10496
            SPB = R1[0:120, SPB_OFF:SPB_OFF + 2048].rearrange("p (a c) -> p a c", c=1024)
            dma("sp", SPB, spool[l].rearrange("s t c -> (s t) c").rearrange("(a p) c -> p a c", p=120), [], ["SPB"], "spb")
            PRE = R1[0:15, PRE_OFF:PRE_OFF + 1024]
            WPv = r1b(WP_OFF, 1024).rearrange("p (g c d) -> p g c d", g=4, c=2)
            for g in range(4):
                dma("pool", WPv[:, g, :, :], w_pool[l, g].rearrange("(c p) d -> p c d", p=128), [], ["WP"], "wp")
            Dv = r1b(DB_OFF, 1152).rearrange("p (c t) -> p c t", t=T)
            wout_pass(0, [WO_H0, WO_R1[0], WO_R1[1], WO_R1[0]], preloaded=3, after_cb0=lambda: u_loads((0,)))
            S.barrier()
            chk(4)
            U78_OFF = 0
            SPB_OFF = 2048
            for ti, t in enumerate((7, 8)):
                u78 = r1f(U78_OFF + ti * 1024, 1024)
                for cb in range(2):
                    wv = wb_view(cb, 512)
                    for k in range(KC):
                        mm(PS[cb][:], HTv[:, k, t * 128:(t + 1) * 128], wv[:, k, :], k == 0, k == KC - 1,
                           [("hT", t), ("wb", cb)], [("ps", cb)])
                    cp("act" if cb == 0 else "dve", u78[:, cb * 512:(cb + 1) * 512], PS[cb][:], [("ps", cb)], [("u78", ti, cb)])
            u7 = r1f(U78_OFF, 1024)
            u8 = r1f(U78_OFF + 1024, 1024)
            u7k = [("u78", 0, 0), ("u78", 0, 1)]
            u8k = [("u78", 1, 0), ("u78", 1, 1)]
            dma("sp", spp[l], u7[113:128, :], u7k, [], "spp")
            dma("sp", cc_u_in[l][0:16, :], u7[112:128, :], u7k, [("ccuin", l)], "ccuin")
            A("pool", lambda e, l=l: e.collective_compute("AllGather", ALU.bypass, replica_groups=GROUPS,
                                                           ins=[cc_u_in[l]], outs=[cc_u_out[l]]),
              reads=[("ccuin", l)], writes=[("ccuout", l)], dma=True, slot=("ccu", l), inc=1)
            dma("sp", sps[l, :, 0:7, :], spool[l, :, 8:15, :], [], [], "sps0")
            for tt_ in range(8):
                srcap = R1[tt_:128:8, U78_OFF + 1024:U78_OFF + 2048]
                dma("sp", sps[l, :, 7 + tt_, :], srcap, u8k, [], "sps1")

            dma("sp", PRE, cc_u_out[l][1:16, :], [("ccuout", l)], ["PRE"], "pre")
            EOFFS = [4096, 0]

            def pool_proj(g, cc):
                ch = 2 * g + cc
                cb, j = ch // 4, ch % 4
                wv = wb_view(cb, 512)
                E = r1f(EOFFS[cc], 1407)
                ek_ = ("E", cc)
                ekw = [ek_] + ([("u78", 0, 0), ("u78", 0, 1), ("u78", 1, 0), ("u78", 1, 1)] if cc == 1 else [])
                Ep = E[:, 0:1039]
                Es = E[:, 1039:1407].rearrange("p (s c) -> p s c", c=23)
                for tb, (c0, cn) in enumerate(TB):
                    pb = PS[tb % 2]
                    for k in range(KC):
                        mm(pb[:, 0:cn], wv[:, k, j * 128:(j + 1) * 128], HTv[:, k, c0:c0 + cn], k == 0, k == KC - 1,
                           htk(tb_tiles(tb)) + [("wb", cb)], [("ps", tb % 2)])
                    if tb < 2:
                        cp("act", Ep[:, 15 + c0:15 + c0 + cn], pb[:, 0:cn], [("ps", tb % 2)], ekw)
                    else:
                        cp("act", Ep[:, 15 + c0:15 + 1024], pb[:, 0:256], [("ps", tb % 2)], [ek_])
                        cp("act", Es[:, :, 15:23], pb[:, 256:384].rearrange("p (s c) -> p s c", c=8), [("ps", tb % 2)], [ek_])

            def pool_windows(g, cc):
                w = 2 << g
                E = r1f(EOFFS[cc], 1407)
                ek_ = ("E", cc)
                Ep = E[:, 0:1039]
                Es = E[:, 1039:1407].rearrange("p (s c) -> p s c", c=23)
                ch = 2 * g + cc
                mm(PS[2][:, 0:15], PRE[:, ch * 128:(ch + 1) * 128], IDF[0:15, 0:15], True, True, ["PRE", "IDF"], [("ps", 2)])
                ts("dve", Ep[:, 0:15], PS[2][:, 0:15], FLAG[:, 0:1], ALU.mult, [("ps", 2), "FLAG"], [ek_])
                for a_ in range(2):
                    mm(PS[3][:, a_ * 120:(a_ + 1) * 120], SPB[:, a_, ch * 128:(ch + 1) * 128], IDF[0:120, 0:120], True, True,
                       ["SPB", "IDF"], [("ps", 3)])
                cp("dve", Es[:, :, 0:15], PS[3][:, 0:240].rearrange("p (s c) -> p s c", c=15), [("ps", 3)], [ek_])
                cur_p, cur_s, curk = Ep, Es, ek_
                bufs = [(TA_OFF, "TA"), (TB_OFF, "TB")]
                sh = 1
                step = 0
                while sh < w:
                    off, nk_ = bufs[step % 2]
                    Nt = r1f(off, 1407)
                    Np = Nt[:, 0:1039]
                    Ns = Nt[:, 1039:1407].rearrange("p (s c) -> p s c", c=23)
                    tt("pool", Np[:, sh:1039], cur_p[:, sh:1039], cur_p[:, 0:1039 - sh], ALU.add, [curk], [nk_])
                    tt("pool", Ns[:, :, sh:23], cur_s[:, :, sh:23], cur_s[:, :, 0:23 - sh], ALU.add, [curk], [nk_])
                    cur_p, cur_s, curk = Np, Ns, nk_
                    sh *= 2
                    step += 1
                stt("dve", Dv[:, cc, 0:1024], cur_p[:, 15:1039], 1.0 / w, Ep[:, 15:1039], ALU.mult, ALU.subtract,
                    [curk, ek_], [("D", cc)])
                stt("dve", Dv[:, cc, 1024:1152].rearrange("p (s c) -> p s c", c=8), cur_s[:, :, 15:23], 1.0 / w,
                    Es[:, :, 15:23], ALU.mult, ALU.subtract, [curk, ek_], [("D", cc)])
                t15 = STAT[:, 32:48]
                tt("dve", t15[:, 0:15], cur_p[:, 15:30], INVC[:, g * 15:(g + 1) * 15], ALU.mult, [curk, "INVC"], ["t15"])
                tt("dve", Dv[:, cc, 0:15], t15[:, 0:15], Ep[:, 15:30], ALU.subtract, ["t15", ek_, ("D", cc)], [("D", cc)])

            def pool_mm(g):
                for dc in range(2):
                    for tb, (c0, cn) in enumerate(TB):
                        pb = PS[4 + tb % 2]
                        for cc in range(2):
                            mm(pb[:, 0:cn], WPv[:, g, cc, dc * 128:(dc + 1) * 128], Dv[:, cc, c0:c0 + cn], cc == 0, cc == 1,
                               ["WP", ("D", cc)], [("ps", 4 + tb % 2)])
                        ts("dve", MTv[:, 2 * g + dc, c0:c0 + cn], pb[:, 0:cn], PSC[:, 2 * g + dc:2 * g + dc + 1], ALU.mult,
                           [("ps", 4 + tb % 2), "PSC"], [("mT", 2 * g + dc, t) for t in tb_tiles(tb)])

            WO_WB = [(WB[1][:, i * 4096:(i + 1) * 4096].rearrange("p (k n) -> p k n", n=512),
                      [("wb1h", i), ("wb", 1), ("wb4", 2 + i)], ("wb1h", i)) for i in range(2)]
            go = [3, 2, 1, 0]
            for gi_, g in enumerate(go):
                pool_proj(g, 0)
                pool_proj(g, 1)
                if gi_ == 1:
                    wout_load(1024, 0, WO_WB[0])
                    wout_load(1024, 1, WO_WB[1])
                if gi_ > 0:
                    pool_mm(go[gi_ - 1])
                pool_windows(g, 0)
                pool_windows(g, 1)
            pool_mm(go[-1])
            S.barrier()
            ffn_loads(l, 0, alias=True, part=0)
            wout_pass(1024, WO_WB, preloaded=2)
            chk(5)

            ffn_loads(l, 0, alias=True, part=1)
            emit_norm(norm_ffn[l:l + 1, :])
            SG_OFF = [8192, 8576]
            ACTv = MT[:].rearrange("p (b f t) -> p b f t", b=2, f=FG)

            def down(gi, tile_done=None):
                b = gi % 2
                wd = r1b(WD_OFF[b], 4096).rearrange("p (f n) -> p f n", n=D)
                for t in range(NT):
                    if tile_done is not None and t > 0:
                        tile_done(t - 1)
                    for cb in range(4):
                        pb = PS[6 + (t * 4 + cb) % 2]
                        for f in range(FG):
                            mm(pb[:], ACTv[:, b, f, t * 128:(t + 1) * 128], wd[:, f, cb * 512:(cb + 1) * 512], f == 0, f == FG - 1,
                               [("act", b, f), ("wd", b)], [("ps", 6 + (t * 4 + cb) % 2)])
                        tt("dve", Xv[:, t, cb * 512:(cb + 1) * 512], Xv[:, t, cb * 512:(cb + 1) * 512], pb[:], ALU.add,
                           [("x", t, cb), ("ps", 6 + (t * 4 + cb) % 2)], [("x", t, cb)])

            for gi in range(NG):
                b = gi % 2
                if gi > 0:
                    ffn_loads(l, gi)
                for f in range(FG):
                    ff = gi * FG + f
                    slot = ffn_slot(ff)
                    wv = gu_view(slot)
                    for tb, (c0, cn) in enumerate(TB):
                        pg, pu = PS[tb], PS[3 + tb]
                        for k in range(KC):
                            mm(pg[:, 0:cn], wv[:, k, 0:128], HTv[:, k, c0:c0 + cn], k == 0, k == KC - 1,
                               htk(tb_tiles(tb)) + [("wb4", slot)], [("ps", tb)])
                        for k in range(KC):
                            mm(pu[:, 0:cn], wv[:, k, 128:256], HTv[:, k, c0:c0 + cn], k == 0, k == KC - 1,
                               htk(tb_tiles(tb)) + [("wb4", slot)], [("ps", 3 + tb)])
                        sg = r1f(SG_OFF[tb % 2], 384)
                        act(sg, pg[:, 0:cn], AF.Silu, [("ps", tb)], [("sg", tb % 2)])
                        tt("dve", ACTv[:, b, f, c0:c0 + cn], sg, pu[:, 0:cn], ALU.mult, [("sg", tb % 2), ("ps", 3 + tb)], [("act", b, f)])
                if gi > 0:
                    down(gi - 1)
            if l + 1 < depth:
                layer_prefetch(l + 1)
                down(NG - 1)
            else:
                GBF = r1f(4096, 2048)
                dma("sp", GBF, norm_final[0, :].partition_broadcast(128), [], ["gbf", ("wd", 1)], "gbc")

                def fin(t):
                    p = t % 2
                    junk = r1b(10240, 1024)
                    act(junk, Xv[:, t, :], AF.Square, xk(t), ["junk", ("ss", p)], scale=1.0 / np.sqrt(D), accum=STAT[:, p:p + 1])
                    act(STAT[:, 2 + p:3 + p], STAT[:, p:p + 1], AF.Ln, [("ss", p)], [("lnv", p)], bias=EPS)
                    act(STAT[:, 4 + p:5 + p], STAT[:, 2 + p:3 + p], AF.Exp, [("lnv", p)], [("rs", p)], scale=-0.5)
                    yt = r1f(6144 + p * D, D)
                    act(yt, Xv[:, t, :], AF.Copy, xk(t) + [("rs", p)], [("yt", p), ("wd", 1), ("sg", 0), ("sg", 1)],
                        scale=STAT[:, 4 + p:5 + p])
                    tt("pool", yt, yt, GBF, ALU.mult, [("yt", p), "gbf"], [("yt", p)])
                    dma("sp", y[t * 128:(t + 1) * 128, :], yt, [("yt", p)], [], ("yout", p))

                down(NG - 1, tile_done=fin)
                fin(NT - 1)
                final_done = True
            chk(6 + l)
        except _Stop:
            pass

        if not final_done:
            emit_norm(norm_final, to_y=True)
        info = S.emit()
    return nc, info


_CACHE = {}


def _consts(core):
    hf = core % 2
    tri = np.zeros((128, 256), np.float32)
    j = np.arange(128)[:, None]
    i = np.arange(128)[None, :]
    tri[:, 0:128] = (j <= i)
    tri[:, 128:256] = (j <= i) & ((j // 8) == (i // 8))
    seqsel = np.zeros((128, 32), np.float32)
    seqsel[:, 0] = 1.0
    for s in range(16):
        seqsel[8 * s:8 * s + 8, 16 + s] = 1.0
    rowmask = np.zeros((128, 16), np.float32)
    for s in range(16):
        rowmask[8 * s:8 * s + 8, s] = 1.0
    flag = np.full((128, 1), float(hf), np.float32)
    invc = np.zeros((128, 60), np.float32)
    for g, w in enumerate((2, 4, 8, 16)):
        for t in range(15):
            cnt = w if hf == 1 else min(w, t + 1)
            invc[:, g * 15 + t] = 1.0 / cnt
    return dict(c_identb=np.eye(128, dtype=np.float32).astype(ml_dtypes.bfloat16), c_identf=np.eye(128, dtype=np.float32),
                c_tri=tri, c_seqsel=seqsel, c_rowmask=rowmask, c_flag=flag, c_invc=invc)


def kernel(x_prompt, x_sample, state_gla, state_pool, norm_mix, w_in, w_gk_up, b_gk, gla_norm, w_pool,
           pool_scale, w_out, norm_ffn, w_gate, w_up, w_down, norm_final):
    f = lambda a: np.ascontiguousarray(np.asarray(a, dtype=np.float32))
    x_prompt, x_sample, state_gla, state_pool = f(x_prompt), f(x_sample), f(state_gla), f(state_pool)
    shared = dict(norm_mix=f(norm_mix), w_in=f(w_in), w_gk_up=f(w_gk_up), b_gk=f(b_gk), gla_norm=f(gla_norm),
                  w_pool=f(w_pool), pool_scale=f(pool_scale), w_out=f(w_out), norm_ffn=f(norm_ffn), w_gate=f(w_gate),
                  w_up=f(w_up), w_down=f(w_down), norm_final=f(norm_final).reshape(1, D))
    if "nc" not in _CACHE:
        import os
        _st = int(os.environ.get("MK_STAGE", "99"))
        _CACHE["nc"] = build_program(stage=_st)
        _CACHE["stage"] = _st
    nc, info = _CACHE["nc"]
    if _CACHE.get("stage", 99) < 6:
        for k_ in ("w_gate", "w_up", "w_down"):
            shared.pop(k_)
    in_maps = []
    for c in range(NCORES):
        b, hf = c // 2, c % 2
        xin = np.concatenate([x_prompt[b, hf * 1024:(hf + 1) * 1024], x_sample[16 * c:16 * c + 16].reshape(128, D)], 0)
        m = dict(shared)
        m["xin"] = np.ascontiguousarray(xin)
        m["sgla"] = np.ascontiguousarray(state_gla[:, 16 * c:16 * c + 16])
        m["spool"] = np.ascontiguousarray(state_pool[:, 16 * c:16 * c + 16])
        m.update(_consts(c))
        in_maps.append(m)
    res = run_bass_kernel_spmd(nc, in_maps, core_ids=list(range(NCORES)))
    y_prompt = np.zeros((4, 2048, D), np.float32)
    y_sample = np.zeros((128, 8, D), np.float32)
    sg_p = np.zeros((2, 4, H, DK, DV), np.float32)
    sp_p = np.zeros((2, 4, 15, 1024), np.float32)
    sg_s = np.zeros((2, 128, H, DK, DV), np.float32)
    sp_s = np.zeros((2, 128, 15, 1024), np.float32)
    for c in range(NCORES):
        r = res.results[c]
        b, hf = c // 2, c % 2
        y_prompt[b, hf * 1024:(hf + 1) * 1024] = r["y"][0:1024]
        y_sample[16 * c:16 * c + 16] = r["y"][1024:].reshape(16, 8, D)
        if hf == 1:
            sg_p[:, b] = r["sgp"]
            sp_p[:, b] = r["spp"]
        sg_s[:, 16 * c:16 * c + 16] = r["sgs"]
        sp_s[:, 16 * c:16 * c + 16] = r["sps"]
    return (y_prompt, y_sample, sg_p, sp_p, sg_s, sp_s)
```
